# Optimizing a Trainium2 kernel written in Bass

```python
import jax, jax.numpy as jnp
from jax import lax
import numpy as np

D_MODEL = 1024
BATCH = 16
SEQ = 4096
DEPTH = 1
DEC_BATCH = 32
DEC_SEQ = 64
PAST_LEN = 4096

CHUNK = 64
D_CONV_A = D_MODEL
D_CONV_B = D_MODEL
CONV_A_WIDTH = 31
CONV_B_WIDTH = 3
D_FF = 4 * D_MODEL
N_IN = 2 * D_CONV_A + 3 * D_CONV_B + 2 * D_MODEL
EPS = 1e-6

kernel_name = "parallel_conformer_shortconv_encoder_step"


def rmsnorm(x, g):
    xf = x.astype(jnp.float32)
    y = xf * lax.rsqrt(jnp.mean(xf * xf, axis=-1, keepdims=True) + EPS)
    return (y * g.astype(jnp.float32)).astype(x.dtype)


def layernorm(x, g, b):
    xf = x.astype(jnp.float32)
    mu = jnp.mean(xf, axis=-1, keepdims=True)
    var = jnp.mean(jnp.square(xf - mu), axis=-1, keepdims=True)
    y = (xf - mu) * lax.rsqrt(var + EPS)
    return (y * g.astype(jnp.float32) + b.astype(jnp.float32)).astype(x.dtype)


def causal_dwconv(x, hist, w):
    width, ch = w.shape
    xx = jnp.concatenate([hist.astype(x.dtype), x], axis=1)
    out = lax.conv_general_dilated(
        xx, w.astype(x.dtype)[:, None, :], window_strides=(1,), padding='VALID',
        dimension_numbers=('NWC', 'WIO', 'NWC'), feature_group_count=ch)
    return out, xx[:, xx.shape[1] - (width - 1):]


def layer(x, hist_a, hist_b, norm1_g, w_in, dw_a, dw_a_b, ln_a_g, ln_a_b, wa_out,
          dw_b, wb_out, gate_bias, w_o, norm2_g, w_up, w_down):
    h = rmsnorm(x, norm1_g)
    proj = jnp.einsum('btd,dn->btn', h, w_in)
    o1 = D_CONV_A
    o2 = o1 + D_CONV_A
    o3 = o2 + D_CONV_B
    o4 = o3 + D_CONV_B
    o5 = o4 + D_CONV_B
    o6 = o5 + D_MODEL
    a_val, a_gate = proj[..., :o1], proj[..., o1:o2]
    g_b, g_c, h_b = proj[..., o2:o3], proj[..., o3:o4], proj[..., o4:o5]
    gate_a = jax.nn.sigmoid(proj[..., o5:o6] + gate_bias[:D_MODEL])
    gate_b = jax.nn.sigmoid(proj[..., o6:] + gate_bias[D_MODEL:])
    u = a_val * jax.nn.sigmoid(a_gate)
    ca, new_a = causal_dwconv(u, hist_a, dw_a)
    za = jax.nn.silu(layernorm(ca + dw_a_b, ln_a_g, ln_a_b))
    y_a = jnp.einsum('btc,cd->btd', za, wa_out)
    cb, new_b = causal_dwconv(g_c * h_b, hist_b, dw_b)
    y_b = jnp.einsum('btc,cd->btd', g_b * cb, wb_out)
    x = x + jnp.einsum('btd,de->bte', gate_a * y_a + gate_b * y_b, w_o)
    h2 = rmsnorm(x, norm2_g)
    hid = jnp.square(jax.nn.relu(jnp.einsum('btd,df->btf', h2, w_up)))
    x = x + jnp.einsum('btf,fd->btd', hid, w_down)
    return x, new_a, new_b


def setup_inputs(seed: int = 0) -> dict:
    key = jax.random.key(seed)
    ks = jax.random.split(key, 20)
    f32 = jnp.float32
    nrm = lambda k, s, sc: jax.random.normal(k, s, f32) * sc
    L = DEPTH
    return {
        "x_prompt": nrm(ks[0], (BATCH, SEQ, D_MODEL), 1.0),
        "x_sample": nrm(ks[1], (DEC_BATCH, DEC_SEQ, D_MODEL), 1.0),
        "cache_conv_a": nrm(ks[2], (L, DEC_BATCH, CONV_A_WIDTH - 1, D_CONV_A), 0.5),
        "cache_conv_b": nrm(ks[3], (L, DEC_BATCH, CONV_B_WIDTH - 1, D_CONV_B), 0.5),
        "norm1_g": 1.0 + nrm(ks[4], (L, D_MODEL), 0.01),
        "w_in": nrm(ks[5], (L, D_MODEL, N_IN), D_MODEL ** -0.5),
        "dw_a": nrm(ks[6], (L, CONV_A_WIDTH, D_CONV_A), CONV_A_WIDTH ** -0.5),
        "dw_a_b": nrm(ks[7], (L, D_CONV_A), 0.01),
        "ln_a_g": 1.0 + nrm(ks[8], (L, D_CONV_A), 0.01),
        "ln_a_b": nrm(ks[9], (L, D_CONV_A), 0.01),
        "wa_out": nrm(ks[10], (L, D_CONV_A, D_MODEL), D_CONV_A ** -0.5),
        "dw_b": nrm(ks[11], (L, CONV_B_WIDTH, D_CONV_B), CONV_B_WIDTH ** -0.5),
        "wb_out": nrm(ks[12], (L, D_CONV_B, D_MODEL), D_CONV_B ** -0.5),
        "gate_bias": nrm(ks[13], (L, 2 * D_MODEL), 0.01),
        "w_o": nrm(ks[14], (L, D_MODEL, D_MODEL), D_MODEL ** -0.5),
        "norm2_g": 1.0 + nrm(ks[15], (L, D_MODEL), 0.01),
        "w_up": nrm(ks[16], (L, D_MODEL, D_FF), D_MODEL ** -0.5),
        "w_down": nrm(ks[17], (L, D_FF, D_MODEL), D_FF ** -0.5),
        "final_norm_g": 1.0 + nrm(ks[18], (D_MODEL,), 0.01),
    }


def reference(x_prompt, x_sample, cache_conv_a, cache_conv_b, norm1_g, w_in, dw_a, dw_a_b,
              ln_a_g, ln_a_b, wa_out, dw_b, wb_out, gate_bias, w_o, norm2_g, w_up, w_down,
              final_norm_g):
    xp, xs = x_prompt, x_sample
    pa, pb, sa, sb = [], [], [], []
    for l in range(DEPTH):
        params = (norm1_g[l], w_in[l], dw_a[l], dw_a_b[l], ln_a_g[l], ln_a_b[l], wa_out[l],
                  dw_b[l], wb_out[l], gate_bias[l], w_o[l], norm2_g[l], w_up[l], w_down[l])
        zero_a = jnp.zeros((xp.shape[0], CONV_A_WIDTH - 1, D_CONV_A), xp.dtype)
        zero_b = jnp.zeros((xp.shape[0], CONV_B_WIDTH - 1, D_CONV_B), xp.dtype)
        xp, na, nb = layer(xp, zero_a, zero_b, *params)
        pa.append(na)
        pb.append(nb)
        xs, na, nb = layer(xs, cache_conv_a[l], cache_conv_b[l], *params)
        sa.append(na)
        sb.append(nb)
    y_prompt = rmsnorm(xp, final_norm_g)
    y_sample = rmsnorm(xs, final_norm_g)
    prompt_conv_a = jnp.stack(pa, axis=0)
    prompt_conv_b = jnp.stack(pb, axis=0)
    sample_conv_a = jnp.stack(sa, axis=0)
    sample_conv_b = jnp.stack(sb, axis=0)
    return (y_prompt, y_sample, prompt_conv_a, prompt_conv_b, sample_conv_a, sample_conv_b)
```

```python
import numpy as np
import concourse.bass as bass
import concourse.mybir as mybir
from concourse.bass_utils import run_bass_kernel_spmd

F32 = mybir.dt.float32
BF16 = mybir.dt.bfloat16
AF = mybir.ActivationFunctionType
ALU = mybir.AluOpType

P = 128
D = 1024
KC = 8
NIN = 7168
DFF = 4096
FC = 32
HA = 30
HB = 2
WA = 31
WB = 3
T = 512
SL = 64
UW = 544
CW = 520
EPS = 1e-6
NV = 42
R_G1, R_DWA, R_DWAB, R_LNG, R_LNB, R_DWB, R_GBA, R_GBB, R_G2 = 0, 1, 32, 33, 34, 35, 38, 39, 40
NSLOT = 4
KPE0 = 16
SEC_OFF = {"val": 0, "gate": 1024, "gB": 2048, "gC": 3072, "hB": 4096, "ga": 5120, "gb": 6144}
WIN_GROUPS = [("gate", 0), ("val", 0), ("gate", 1), ("val", 1), ("gC", 0), ("hB", 0), ("gC", 1), ("hB", 1),
              ("gB", 0), ("gB", 1), ("ga", 0), ("ga", 1), ("gb", 0), ("gb", 1)]


class Sched:
    def __init__(self):
        self.streams = {e: [] for e in ("pe", "act", "dve", "pool", "sp")}
        self.cnt = {}
        self.waited = {e: {} for e in self.streams}
        self.last_w = {}
        self.readers = {}
        self.eng_sem = {"pe": "s_pe", "act": "s_act", "dve": "s_dve", "pool": "s_pool"}
        self.sem_names = set(self.eng_sem.values())

    def op(self, eng, fn, reads=(), writes=(), dma_sem=None):
        deps = {}

        def add(tok):
            if tok is None:
                return
            s, v = tok
            if deps.get(s, 0) < v:
                deps[s] = v

        for k in reads:
            add(self.last_w.get(k))
        for k in writes:
            add(self.last_w.get(k))
            for t in self.readers.get(k, ()):
                add(t)
        w = self.waited[eng]
        if eng == "pe":
            deps.pop("s_pe", None)
        for s, v in deps.items():
            if w.get(s, 0) < v:
                w[s] = v
                self.streams[eng].append(("wait", s, v))
        if dma_sem is not None:
            sem, inc = dma_sem, 16
        else:
            sem, inc = self.eng_sem[eng], 1
        self.sem_names.add(sem)
        self.cnt[sem] = self.cnt.get(sem, 0) + inc
        tok = (sem, self.cnt[sem])
        self.streams[eng].append(("op", fn, sem, inc))
        for k in writes:
            self.last_w[k] = tok
            self.readers[k] = []
        for k in reads:
            self.readers.setdefault(k, []).append(tok)
        return tok

    def wait_all(self, eng, sems):
        for s in sems:
            v = self.cnt.get(s, 0)
            if v and self.waited[eng].get(s, 0) < v:
                self.waited[eng][s] = v
                self.streams[eng].append(("wait", s, v))


def make_tiles(NP, PL, NS):
    tiles = []
    tok = 0
    for b in range(NP):
        nt = PL // T
        for j in range(nt):
            tiles.append(dict(kind="prompt", seq=b, first=(j == 0), last=(j == nt - 1), tok0=tok,
                              NT=T, nsub=T // P, nseg=1, seglen=T))
            tok += T
    if NS:
        tiles.append(dict(kind="samp", seq=NP, first=True, last=True, tok0=NP * PL,
                          NT=NS * SL, nsub=NS * SL // P, nseg=NS, seglen=SL))
    for i, t in enumerate(tiles):
        t["idx"] = i
        t["buf"] = i % 2
        t["kpe"] = KPE0 if i == 0 else 0
    return tiles


def build(NP, PL, NS, debug=()):
    assert PL % T == 0 and (NS * SL) % P == 0
    nc = bass.Bass("TRN2", target_bir_lowering=False)
    NTOK = NP * PL + NS * SL
    NSEQ = NP + NS
    dt_in = lambda name, shape: nc.dram_tensor(name, shape, F32, kind="ExternalInput")
    xin = dt_in("xin", [NTOK, D])
    hca = dt_in("hca", [max(NS, 1) * HA, D])
    hcb = dt_in("hcb", [max(NS, 1) * HB, D])
    vecs = dt_in("vecs", [NV, D])
    gfin_d = dt_in("gfin", [1, D])
    ident_d = dt_in("ident", [P, P])
    w_in_d = dt_in("w_in", [D, NIN])
    wa_d = dt_in("wa_out", [D, D])
    wb_d = dt_in("wb_out", [D, D])
    wo_d = dt_in("w_o", [D, D])
    wup_d = dt_in("w_up", [D, DFF])
    wdn_d = dt_in("w_down", [DFF, D])
    yout = nc.dram_tensor("yout", [NTOK, D], F32, kind="ExternalOutput")
    oca = nc.dram_tensor("oca", [NSEQ * HA, D], F32, kind="ExternalOutput")
    ocb = nc.dram_tensor("ocb", [NSEQ * HB, D], F32, kind="ExternalOutput")
    sc = lambda name, shape: nc.dram_tensor(name, shape, BF16, kind="Internal")
    w_in_s = sc("w_in_s", [D, NIN])
    wa_s = sc("wa_s", [D, D])
    wb_s = sc("wb_s", [D, D])
    wo_s = sc("wo_s", [D, D])
    wup_s = sc("wup_s", [D, DFF])
    wdn_s = sc("wdn_s", [DFF, D])
    dga_s = sc("dga_s", [KC, P, WA * P])
    dbg_out = {}
    for name, shape in debug:
        dbg_out[name] = nc.dram_tensor("dbg_" + name, list(shape), F32, kind="ExternalOutput")

    tiles = make_tiles(NP, PL, NS)
    S = Sched()

    import contextlib
    es = contextlib.ExitStack()
    sb = lambda name, shape, dt: es.enter_context(nc.sbuf_tensor(name, shape, dt))
    ident_f = sb("ident_f", [P, P], F32)
    ident_b = sb("ident_b", [P, P], BF16)
    ones_b = sb("ones_b", [P, P], BF16)
    cvec = sb("cvec", [P, KC, NV], F32)
    gfin = sb("gfin_sb", [P, D], F32)
    dgb = sb("dgb", [P, KC, WB, P], BF16)
    xt = sb("xt", [P, 2, T // P, D], F32)
    xn = sb("xn", [P, T // P, D], BF16)
    hT = sb("hT", [P, KC, T], BF16)
    h2T = sb("h2T", [P, KC, T], BF16)
    ub = sb("ub", [P, KC, UW], BF16)
    cbn = sb("cbn", [P, KC, CW], BF16)
    gBm = sb("gBm", [P, KC, T], BF16)
    ga = sb("ga", [P, KC, T], BF16)
    gb = sb("gb", [P, KC, T], BF16)
    hid = sb("hid", [P, FC, T], BF16)
    tf = sb("tf", [P, 6, T], F32)
    tb = sb("tb", [P, 6, T], BF16)
    junk = tb[:, 4:6, :].rearrange("p a b -> p (a b)")
    acc = sb("acc", [P, 2, T], F32)
    acc_b16 = acc.bitcast(BF16)
    accb = sb("accb", [P, KC, T], BF16)
    mhalf = sb("mhalf", [P, 4], F32)
    dummy = sb("dummy_act", [P, 2], F32)
    stat = sb("stat", [P, 3, 3, 4], F32)
    hco = sb("hco", [P, D], F32)
    uo = tf[:, 4:6, :].rearrange("p a (c w) -> p (a c) w", w=P)
    ring = sb("ring", [P, NSLOT, 4096], BF16)
    ps = es.enter_context(nc.psum_tensor("ps", [P, 8, 512], F32))
    psb = ps.bitcast(BF16)

    bank_state = {"next": 0, "held": set()}

    def next_bank(hold=False):
        while True:
            b = bank_state["next"]
            bank_state["next"] = (b + 1) % 8
            if b not in bank_state["held"]:
                break
        if hold:
            bank_state["held"].add(b)
        return b

    def unhold(b):
        bank_state["held"].discard(b)

    def wuses_for_tile():
        u = []
        for g, (sec, h) in enumerate(WIN_GROUPS):
            c0 = SEC_OFF[sec] + h * 512
            u.append((("w_in", g), w_in_s.ap()[:, c0:c0 + 512].rearrange("(k p) n -> p k n", p=P), "k"))
        for c in range(KC if KPE0 else 0):
            u.append((("dga", c), dga_s.ap()[c][:, 0:KPE0 * P], "flat"))
        for h in range(2):
            u.append((("wb", h), wb_s.ap()[:, h * 512:(h + 1) * 512].rearrange("(k p) n -> p k n", p=P), "k"))
        for h in range(2):
            u.append((("wa", h), wa_s.ap()[:, h * 512:(h + 1) * 512].rearrange("(k p) n -> p k n", p=P), "k"))
        for j in range(2):
            u.append((("wo", j), wo_s.ap()[:, j * 512:(j + 1) * 512].rearrange("(k p) n -> p k n", p=P), "k"))
        for g in range(8):
            u.append((("wup", g), wup_s.ap()[:, g * 512:(g + 1) * 512].rearrange("(k p) n -> p k n", p=P), "k"))
        for j in range(2):
            for q in range(4):
                u.append((("wdn", j, q),
                          wdn_s.ap()[q * 1024:(q + 1) * 1024, j * 512:(j + 1) * 512].rearrange("(k p) n -> p k n", p=P),
                          "k"))
        return u

    per_tile = wuses_for_tile()
    sect = {}
    for u in per_tile:
        sect.setdefault(u[0][0], []).append(u)
    wseq = list(sect["w_in"])
    for ti in range(len(tiles)):
        wseq += (sect.get("dga", []) if ti == 0 else []) + sect["wb"] + sect["wa"] + sect["wo"]
        if ti + 1 < len(tiles):
            wseq += sect["w_in"]
        wseq += sect["wup"] + sect["wdn"]
    wstate = {"next": 0}

    def ring_view(slot, kind):
        if kind == "k":
            return ring[:, slot, :].rearrange("p (k n) -> p k n", n=512)
        return ring[:, slot, 0:KPE0 * P]

    def record_wload(i):
        key, src, kind = wseq[i]
        slot = i % NSLOT
        dst = ring_view(slot, kind)
        assert ("wsc",) + key in S.last_w, key
        S.op("sp", lambda e, dst=dst, src=src: e.dma_start(out=dst, in_=src),
             reads=[("wsc",) + key], writes=[("ring", slot)], dma_sem="s_w%d" % slot)

    def get_w(expect):
        i = wstate["next"]
        wstate["next"] += 1
        key, src, kind = wseq[i]
        assert key[0] == expect, (key, expect)
        slot = i % NSLOT
        return i, slot, ring_view(slot, kind)

    def release_w(i):
        if i + NSLOT < len(wseq):
            record_wload(i + NSLOT)

    def dump(name, src_ap, keys):
        if name in dbg_out:
            S.op("pool", lambda e, o=dbg_out[name], s=src_ap: e.dma_start(out=o.ap(), in_=s),
                 reads=keys, dma_sem="s_dbg")

    S.op("sp", lambda e: e.dma_start(out=ident_f[:], in_=ident_d.ap()), writes=["ident_f"], dma_sem="s_const")
    S.op("sp", lambda e: e.dma_start(out=hco[0:NV, :], in_=vecs.ap()), writes=["hco"], dma_sem="s_const")
    S.op("sp", lambda e: e.dma_start(out=gfin[:], in_=bass.AP(gfin_d, 0, [[0, P], [1, D]])),
         writes=["gfin"], dma_sem="s_const")
    ctok = ("s_const", S.cnt["s_const"])
    for k in ("ident_f", "hco", "gfin"):
        S.last_w[k] = ctok

    def load_x(t):
        buf, ns, t0 = t["buf"], t["nsub"], t["tok0"]
        src = xin.ap()[t0:t0 + t["NT"], :].rearrange("(s p) d -> p s d", p=P)
        S.op("pool", lambda e, src=src, buf=buf, ns=ns: e.dma_start(out=xt[:, buf, 0:ns, :], in_=src),
             writes=[("xt", buf, s) for s in range(ns)], dma_sem="s_x%d" % buf)

    load_x(tiles[0])
    if len(tiles) > 1:
        load_x(tiles[1])

    S.op("pool", lambda e: e.memset(mhalf[:], -0.5), writes=["mhalf"])
    S.op("pool", lambda e: e.memset(dummy[:], 1.0), writes=["dummy"])

    def cast(key, dst, src):
        S.op("pool", lambda e, dst=dst, src=src: e.dma_start(out=dst, in_=src),
             writes=[("wsc",) + key], dma_sem="s_c_" + "_".join(str(x) for x in key))

    def casts_win():
        for g, (sec, h) in enumerate(WIN_GROUPS):
            c0 = SEC_OFF[sec] + h * 512
            cast(("w_in", g), w_in_s.ap()[:, c0:c0 + 512], w_in_d.ap()[:, c0:c0 + 512])

    def casts_attn():
        for h in range(2):
            cast(("wb", h), wb_s.ap()[:, h * 512:(h + 1) * 512], wb_d.ap()[:, h * 512:(h + 1) * 512])
        for h in range(2):
            cast(("wa", h), wa_s.ap()[:, h * 512:(h + 1) * 512], wa_d.ap()[:, h * 512:(h + 1) * 512])
        for j in range(2):
            cast(("wo", j), wo_s.ap()[:, j * 512:(j + 1) * 512], wo_d.ap()[:, j * 512:(j + 1) * 512])

    def casts_up():
        for g in range(8):
            cast(("wup", g), wup_s.ap()[:, g * 512:(g + 1) * 512], wup_d.ap()[:, g * 512:(g + 1) * 512])

    def casts_down():
        for j in range(2):
            for q in range(4):
                cast(("wdn", j, q), wdn_s.ap()[q * 1024:(q + 1) * 1024, j * 512:(j + 1) * 512],
                     wdn_d.ap()[q * 1024:(q + 1) * 1024, j * 512:(j + 1) * 512])

    S.op("dve", lambda e: e.tensor_copy(out=ident_b[:], in_=ident_f[:]), reads=["ident_f"], writes=["ident_b"])
    S.op("dve", lambda e: e.memset(ones_b[:], 1.0 / D), writes=["ones_b"])

    bk = next_bank()

    def tr_vecs(e, bk=bk):
        ins = None
        for c in range(KC):
            ins = e.transpose(out=ps[:, bk, c * 64:c * 64 + NV], in_=hco[0:NV, c * P:(c + 1) * P],
                              identity=ident_f[0:NV, 0:NV])
        return ins

    S.op("pe", tr_vecs, reads=["hco", "ident_f"], writes=[("ps", bk)])
    S.op("dve", lambda e, bk=bk: e.tensor_copy(
        out=cvec[:], in_=ps[:, bk, :].rearrange("p (c r) -> p c r", r=64)[:, :, 0:NV]),
        reads=[("ps", bk)], writes=["cvec"])

    def build_dgb(e):
        ins = None
        for c in range(KC):
            for k in range(WB):
                ins = e.tensor_scalar(out=dgb[:, c, k, :], in0=ident_f[:], scalar1=cvec[:, c, R_DWB + k:R_DWB + k + 1],
                                      scalar2=None, op0=ALU.mult)
        return ins

    S.op("dve", build_dgb, reads=["cvec", "ident_f"], writes=["dgb"])

    def build_diag_a():
        for c in range(KC):
            st = c % 4
            stg = hid[:, st * 8:(st + 1) * 8, :].rearrange("p a b -> p (a b)")[:, 0:KPE0 * P]
            keys = [("hid", f) for f in range(st * 8, st * 8 + 8)]

            def build_dga(e, c=c, stg=stg):
                ins = None
                for k in range(KPE0):
                    ins = e.tensor_scalar(out=stg[:, k * P:(k + 1) * P], in0=ident_f[:],
                                          scalar1=cvec[:, c, R_DWA + k:R_DWA + k + 1], scalar2=None, op0=ALU.mult)
                return ins

            S.op("dve", build_dga, reads=["cvec", "ident_f"], writes=keys)
            S.op("sp", lambda e, c=c, stg=stg: e.dma_start(out=dga_s.ap()[c][:, 0:KPE0 * P], in_=stg),
                 reads=keys, writes=[("wsc", "dga", c)], dma_sem="s_dg%d" % c)

    def rstd_ops(stage, s):
        ssap = stat[:, stage, 0, s:s + 1]
        sdap = stat[:, stage, 1, s:s + 1]
        rsap = stat[:, stage, 2, s:s + 1]
        S.op("pool", lambda e: e.tensor_scalar(out=sdap, in0=ssap, scalar1=1.0 / D, scalar2=EPS, op0=ALU.mult,
                                               op1=ALU.add),
             reads=[("st", stage, 0, s)], writes=[("st", stage, 1, s)])
        S.op("pool", lambda e: e.tensor_tensor(out=rsap, in0=sdap, in1=mhalf[:, 0:1], op=ALU.pow),
             reads=[("st", stage, 1, s), "mhalf"], writes=[("st", stage, 2, s)])
        return rsap

    def rms_part(t, stage):
        buf, ns = t["buf"], t["nsub"]
        rs = []
        for s in range(ns):
            ssap = stat[:, stage, 0, s:s + 1]
            S.op("act", lambda e, s=s, ssap=ssap: e.activation(out=xn[:, s, :], in_=xt[:, buf, s, :], func=AF.Square,
                                                              accum_out=ssap),
                 reads=[("xt", buf, s)], writes=[("st", stage, 0, s), ("xn", s)])
            rs.append(rstd_ops(stage, s))
        for s in range(ns):
            S.op("act", lambda e, s=s, rsap=rs[s]: e.activation(out=xn[:, s, :], in_=xt[:, buf, s, :], func=AF.Copy,
                                                               scale=rsap),
                 reads=[("xt", buf, s), ("st", stage, 2, s)], writes=[("xn", s)])

    def tr_part(t, grow, dst, dkey):
        ns, NT = t["nsub"], t["NT"]
        for q in range(KC // 2):
            bk = next_bank()

            def trs(e, q=q, bk=bk):
                ins = None
                for kk in range(2):
                    k = 2 * q + kk
                    for s in range(ns):
                        ins = e.transpose(out=psb[:, bk, kk * 512 + s * P:kk * 512 + (s + 1) * P],
                                          in_=xn[:, s, k * P:(k + 1) * P], identity=ident_b[:])
                return ins

            S.op("pe", trs, reads=[("xn", s) for s in range(ns)] + ["ident_b"], writes=[("ps", bk)])
            for kk in range(2):
                k = 2 * q + kk
                S.op("act", lambda e, k=k, kk=kk, bk=bk: e.activation(
                    out=dst[:, k, 0:NT], in_=psb[:, bk, kk * 512:kk * 512 + NT], func=AF.Copy,
                    scale=cvec[:, k, grow:grow + 1]),
                    reads=[("ps", bk), "cvec"], writes=[(dkey, k)])

    def uview(t, c, off, width):
        if t["nseg"] == 1:
            return ub[:, c, off:off + width]
        L = HA + t["seglen"]
        return ub[:, c, 0:t["nseg"] * L].rearrange("p (s j) -> p s j", j=L)[:, :, off:off + width]

    def cview(t, c, off, width):
        if t["nseg"] == 1:
            return cbn[:, c, off:off + width]
        L = HB + t["seglen"]
        return cbn[:, c, 0:t["nseg"] * L].rearrange("p (s j) -> p s j", j=L)[:, :, off:off + width]

    def pview(t, ap2d):
        if t["nseg"] == 1:
            return ap2d
        return ap2d.rearrange("p (s j) -> p s j", j=t["seglen"])

    def tail(t, ap2d, n):
        sl = t["seglen"]
        if t["nseg"] == 1:
            return ap2d[:, sl - n:sl]
        return ap2d.rearrange("p (s j) -> p s j", j=sl)[:, :, sl - n:sl]

    conv_q = []

    def queue_conv(t, pairs):
        sl, NT = t["seglen"], t["NT"]
        kpe = t["kpe"]
        taps = list(range(kpe, WA))
        for pair in pairs:
            for ti, k in enumerate(taps):
                for a in range(2):
                    c = 2 * pair + a
                    accv = pview(t, acc[:, a, 0:NT])
                    wk = cvec[:, c, R_DWA + k:R_DWA + k + 1]
                    if ti == 0 and kpe == 0:
                        fn = lambda e, c=c, k=k, accv=accv, wk=wk: e.tensor_scalar(
                            out=accv, in0=uview(t, c, k, sl), scalar1=wk, scalar2=cvec[:, c, R_DWAB:R_DWAB + 1],
                            op0=ALU.mult, op1=ALU.add)
                        conv_q.append((fn, [("ub", c), "cvec"], [("acc", a)]))
                    elif ti == 0:
                        fn = lambda e, c=c, k=k, accv=accv, wk=wk: e.tensor_scalar(
                            out=accv, in0=uview(t, c, k, sl), scalar1=wk, scalar2=None, op0=ALU.mult)
                        conv_q.append((fn, [("ub", c), "cvec"], [("acc", a)]))
                    elif ti < len(taps) - 1:
                        fn = lambda e, c=c, k=k, accv=accv, wk=wk: e.scalar_tensor_tensor(
                            out=accv, in0=uview(t, c, k, sl), scalar=wk, in1=accv, op0=ALU.mult, op1=ALU.add)
                        conv_q.append((fn, [("ub", c), ("acc", a), "cvec"], [("acc", a)]))
                    else:
                        fn = lambda e, c=c, k=k, accv=accv, wk=wk: e.scalar_tensor_tensor(
                            out=pview(t, accb[:, c, 0:NT]), in0=uview(t, c, k, sl), scalar=wk, in1=accv,
                            op0=ALU.mult, op1=ALU.add)
                        conv_q.append((fn, [("ub", c), ("acc", a), "cvec"], [("accb", c)]))

    def pump(n):
        for _ in range(min(n, len(conv_q))):
            fn, rd, wr = conv_q.pop(0)
            S.op("dve", fn, reads=rd, writes=wr)

    def stage_hist(t):
        ukeys = [("ub", c) for c in range(KC)]
        ckeys = [("cbn", c) for c in range(KC)]
        if t["kind"] == "prompt":
            if t["first"]:
                S.op("pool", lambda e: e.memset(ub[:, :, 0:HA], 0.0), writes=ukeys)
                S.op("pool", lambda e: e.memset(cbn[:, :, 0:HB], 0.0), writes=ckeys)
            else:
                S.op("pool", lambda e: e.tensor_copy(out=ub[:, :, 0:HA], in_=ub[:, :, T:T + HA]),
                     reads=ukeys, writes=ukeys)
                S.op("pool", lambda e: e.tensor_copy(out=cbn[:, :, 0:HB], in_=cbn[:, :, T:T + HB]),
                     reads=ckeys, writes=ckeys)
        else:
            ns = t["nseg"]
            S.op("pool", lambda e: e.dma_start(out=hco[0:ns * HA, :], in_=hca.ap()), writes=["hco"], dma_sem="s_hc")
            S.op("pool", lambda e: e.dma_start(out=hco[P - ns * HB:P, :], in_=hcb.ap()), writes=["hco2"],
                 dma_sem="s_hc")
            tok = ("s_hc", S.cnt["s_hc"])
            S.last_w["hco"] = tok
            S.last_w["hco2"] = tok
            for half in range(2):
                bk = next_bank()

                def trh(e, half=half, bk=bk):
                    ins = None
                    for cc in range(4):
                        c = half * 4 + cc
                        ins = e.transpose(out=ps[:, bk, cc * P:(cc + 1) * P], in_=hco[:, c * P:(c + 1) * P],
                                          identity=ident_f[:])
                    return ins

                S.op("pe", trh, reads=["hco", "hco2", "ident_f"], writes=[("ps", bk)])
                for cc in range(4):
                    c = half * 4 + cc
                    S.op("dve", lambda e, c=c, cc=cc, bk=bk: e.tensor_copy(
                        out=uview(t, c, 0, HA),
                        in_=ps[:, bk, cc * P:cc * P + ns * HA].rearrange("p (s j) -> p s j", j=HA)),
                        reads=[("ps", bk)], writes=[("ub", c)])
                    S.op("dve", lambda e, c=c, cc=cc, bk=bk: e.tensor_copy(
                        out=cview(t, c, 0, HB),
                        in_=ps[:, bk, cc * P + P - ns * HB:(cc + 1) * P].rearrange("p (s j) -> p s j", j=HB)),
                        reads=[("ps", bk)], writes=[("cbn", c)])

    sig_slot = {}
    gc_slot = {}

    def stage_win(t, g0, g1):
        NT, nseg, sl = t["NT"], t["nseg"], t["seglen"]
        last = t["last"]
        for g in range(g0, g1):
            sec, h = WIN_GROUPS[g]
            wi, slot, wv = get_w("w_in")
            for i in range(4):
                c = h * 4 + i
                bk = next_bank()

                def mm(e, i=i, bk=bk, wv=wv):
                    ins = None
                    for kc in range(KC):
                        ins = e.matmul(ps[:, bk, 0:NT], lhsT=wv[:, kc, i * P:(i + 1) * P], rhs=hT[:, kc, 0:NT],
                                       start=(kc == 0), stop=(kc == KC - 1))
                    return ins

                S.op("pe", mm, reads=[("ring", slot)] + [("hT", k) for k in range(KC)], writes=[("ps", bk)])
                pin = ps[:, bk, 0:NT]
                if sec == "gate":
                    sl_i = i
                    sig_slot[c] = sl_i
                    S.op("act", lambda e, pin=pin, sl_i=sl_i: e.activation(out=tf[:, sl_i, 0:NT], in_=pin,
                                                                          func=AF.Sigmoid),
                         reads=[("ps", bk)], writes=[("tf", sl_i)])
                elif sec == "val":
                    sl_i = sig_slot[c]
                    S.op("dve", lambda e, pin=pin, sl_i=sl_i, c=c: e.tensor_tensor(
                        out=uview(t, c, HA, sl), in0=pview(t, pin), in1=pview(t, tf[:, sl_i, 0:NT]), op=ALU.mult),
                        reads=[("ps", bk), ("tf", sl_i)], writes=[("ub", c)])
                    if last:
                        S.op("dve", lambda e, pin=pin, sl_i=sl_i, c=c: e.tensor_tensor(
                            out=(uo[:, c, 0:HA] if nseg == 1 else
                                 uo[:, c, :].rearrange("p (s j) -> p s j", j=32)[:, :, 0:HA]),
                            in0=tail(t, pin, HA), in1=tail(t, tf[:, sl_i, 0:NT], HA),
                            op=ALU.mult),
                            reads=[("ps", bk), ("tf", sl_i)], writes=[("tf", 4 + c // 4)])
                elif sec == "gC":
                    sl_i = i
                    gc_slot[c] = sl_i
                    S.op("act", lambda e, pin=pin, sl_i=sl_i: e.activation(out=tf[:, sl_i, 0:NT], in_=pin,
                                                                          func=AF.Copy),
                         reads=[("ps", bk)], writes=[("tf", sl_i)])
                elif sec == "hB":
                    sl_i = gc_slot[c]
                    S.op("dve", lambda e, pin=pin, sl_i=sl_i, c=c: e.tensor_tensor(
                        out=cview(t, c, HB, sl), in0=pview(t, pin), in1=pview(t, tf[:, sl_i, 0:NT]), op=ALU.mult),
                        reads=[("ps", bk), ("tf", sl_i)], writes=[("cbn", c)])
                    if last:
                        S.op("dve", lambda e, pin=pin, sl_i=sl_i, c=c: e.tensor_tensor(
                            out=(uo[:, c, HA:HA + HB] if nseg == 1 else
                                 uo[:, c, :].rearrange("p (s j) -> p s j", j=32)[:, :, HA:HA + HB]),
                            in0=tail(t, pin, HB), in1=tail(t, tf[:, sl_i, 0:NT], HB),
                            op=ALU.mult),
                            reads=[("ps", bk), ("tf", sl_i)], writes=[("tf", 4 + c // 4)])
                elif sec == "gB":
                    S.op("act", lambda e, pin=pin, c=c: e.activation(out=gBm[:, c, 0:NT], in_=pin, func=AF.Copy),
                         reads=[("ps", bk)], writes=[("gBm", c)])
                else:
                    dst = ga if sec == "ga" else gb
                    row = R_GBA if sec == "ga" else R_GBB
                    S.op("act", lambda e, pin=pin, c=c, dst=dst, row=row: e.activation(
                        out=dst[:, c, 0:NT], in_=pin, func=AF.Sigmoid, bias=cvec[:, c, row:row + 1]),
                        reads=[("ps", bk), "cvec"], writes=[(sec, c)])
            release_w(wi)
            if g == 1:
                queue_conv(t, (0, 1))
            if g == 3:
                queue_conv(t, (2, 3))
            if g >= 2:
                if t["idx"] == 0:
                    pump(12 if sec in ("hB", "val") else 16)
                else:
                    pump(7 if sec in ("hB", "val") else 10)

    def stage_cache_out(t):
        nseg = t["nseg"]
        W = 32 * nseg
        for half in range(2):
            bk = next_bank()

            def tro(e, half=half, bk=bk):
                ins = None
                for cc in range(4):
                    c = half * 4 + cc
                    ins = e.transpose(out=ps[0:W, bk, cc * P:(cc + 1) * P], in_=uo[:, c, 0:W], identity=ident_f[:])
                return ins

            S.op("pe", tro, reads=[("tf", 4 + half), "ident_f"], writes=[("ps", bk)])
            S.op("dve", lambda e, half=half, bk=bk: e.tensor_copy(out=hco[0:W, half * 512:(half + 1) * 512],
                                                                 in_=ps[0:W, bk, :]),
                 reads=[("ps", bk)], writes=[("hcoh", half)])
        keys = [("hcoh", 0), ("hcoh", 1)]
        for sg in range(nseg):
            seq = t["seq"] + sg
            S.op("pool", lambda e, sg=sg, seq=seq: e.dma_start(out=oca.ap()[seq * HA:(seq + 1) * HA, :],
                                                               in_=hco[sg * 32:sg * 32 + HA, :]),
                 reads=keys, dma_sem="s_oc")
            S.op("pool", lambda e, sg=sg, seq=seq: e.dma_start(out=ocb.ap()[seq * HB:(seq + 1) * HB, :],
                                                               in_=hco[sg * 32 + HA:sg * 32 + HA + HB, :]),
                 reads=keys, dma_sem="s_oc")
        tok = ("s_oc", S.cnt["s_oc"])
        for k in keys + ["hco", "hco2"]:
            S.readers.setdefault(k, []).append(tok)

    def cab_of(t):
        if t["kpe"] == 0:
            return (lambda c: accb[:, c, :]), (lambda c: ("accb", c))
        return (lambda c: hid[:, c, :]), (lambda c: ("hid", c))

    def convb_block(t):
        NT, sl = t["NT"], t["seglen"]
        for c in range(KC):
            bk2 = next_bank()

            def mmb2(e, c=c, bk2=bk2):
                ins = None
                for k in range(WB):
                    ins = e.matmul(pview(t, ps[:, bk2, 0:NT]), lhsT=dgb[:, c, k, :], rhs=cview(t, c, k, sl),
                                   start=(k == 0), stop=(k == WB - 1))
                return ins

            S.op("pe", mmb2, reads=["dgb", ("cbn", c)], writes=[("ps", bk2)])
            S.op("dve", lambda e, c=c, bk2=bk2: e.tensor_tensor(out=hid[:, 16 + c, 0:NT], in0=ps[:, bk2, 0:NT],
                                                               in1=gBm[:, c, 0:NT], op=ALU.mult),
                 reads=[("ps", bk2), ("gBm", c)], writes=[("hid", 16 + c)])

    def pre_squares(t):
        NT = t["NT"]
        pump(len(conv_q))
        for c in range(6):
            S.op("act", lambda e, c=c: e.activation(out=tb[:, c, 0:NT], in_=accb[:, c, 0:NT], func=AF.Square),
                 reads=[("accb", c)], writes=[("tb", c)])
        for c in (6, 7):
            S.op("act", lambda e, c=c: e.activation(out=acc_b16[:, c - 6, 0:NT], in_=accb[:, c, 0:NT],
                                                   func=AF.Square),
                 reads=[("accb", c)], writes=[("acc", c - 6)])
        t["presq"] = True

    def stage_conv(t):
        NT, nseg, sl = t["NT"], t["nseg"], t["seglen"]
        kpe = t["kpe"]
        cabv, cabk = cab_of(t)
        per_pair = 2 * (WA - kpe)
        n_q0 = len(conv_q)
        bM = next_bank(hold=True)
        bE = next_bank(hold=True)
        pend = None

        def stats(c, sq_slot, sq_ap=None, sq_key=None):
            if sq_ap is None:
                sq_ap, sq_key = tb[:, sq_slot, 0:NT], ("tb", sq_slot)
            S.op("pe", lambda e, c=c: e.matmul(ps[:, bM, 0:NT], lhsT=ones_b[:], rhs=cabv(c)[:, 0:NT],
                                               start=(c == 0), stop=(c == KC - 1)),
                 reads=[cabk(c), "ones_b"], writes=[("ps", bM)])
            S.op("pe", lambda e, c=c, sq_ap=sq_ap: e.matmul(ps[:, bE, 0:NT], lhsT=ones_b[:], rhs=sq_ap,
                                                           start=(c == 0), stop=(c == KC - 1)),
                 reads=[sq_key, "ones_b"], writes=[("ps", bE)])

        if kpe == 0 and t.get("presq"):
            pump(len(conv_q))
            for c in range(KC):
                if c >= 6:
                    stats(c, None, acc_b16[:, c - 6, 0:NT], ("acc", c - 6))
                else:
                    stats(c, c)
            t["convb_later"] = True
            return bM, bE

        for c in range(KC):
            need_left = max(0, (KC // 2 - 1 - c // 2) * per_pair)
            if len(conv_q) > need_left:
                pump(len(conv_q) - need_left)
            sq_slot = c % 3
            if kpe == 0:
                S.op("act", lambda e, c=c, sq_slot=sq_slot: e.activation(out=tb[:, sq_slot, 0:NT],
                                                                        in_=accb[:, c, 0:NT], func=AF.Square),
                     reads=[("accb", c)], writes=[("tb", sq_slot)])
            else:
                wi, slot, wv = get_w("dga")
                bk = next_bank()

                def mma(e, c=c, bk=bk, wv=wv):
                    for k in range(kpe):
                        e.matmul(pview(t, ps[:, bk, 0:NT]), lhsT=wv[:, k * P:(k + 1) * P], rhs=uview(t, c, k, sl),
                                 start=(k == 0), stop=False)
                    return e.matmul(ps[:, bk, 0:NT], lhsT=ident_b[:], rhs=accb[:, c, 0:NT], start=False, stop=True)

                S.op("pe", mma, reads=[("ring", slot), ("ub", c), ("accb", c), "ident_b"], writes=[("ps", bk)])
                release_w(wi)
                sq_slot = c % 3
                S.op("act", lambda e, c=c, bk=bk: e.activation(out=hid[:, c, 0:NT], in_=ps[:, bk, 0:NT], func=AF.Identity,
                                                              bias=cvec[:, c, R_DWAB:R_DWAB + 1]),
                     reads=[("ps", bk), "cvec"], writes=[("hid", c)])
                S.op("act", lambda e, c=c, bk=bk, sq_slot=sq_slot: e.activation(
                    out=tb[:, sq_slot, 0:NT], in_=ps[:, bk, 0:NT], func=AF.Square, bias=cvec[:, c, R_DWAB:R_DWAB + 1]),
                    reads=[("ps", bk), "cvec"], writes=[("tb", sq_slot)])
            bk2 = next_bank()

            def mmb(e, c=c, bk2=bk2):
                ins = None
                for k in range(WB):
                    ins = e.matmul(pview(t, ps[:, bk2, 0:NT]), lhsT=dgb[:, c, k, :], rhs=cview(t, c, k, sl),
                                   start=(k == 0), stop=(k == WB - 1))
                return ins

            S.op("pe", mmb, reads=["dgb", ("cbn", c)], writes=[("ps", bk2)])
            S.op("dve", lambda e, c=c, bk2=bk2: e.tensor_tensor(out=hid[:, 16 + c, 0:NT], in0=ps[:, bk2, 0:NT],
                                                               in1=gBm[:, c, 0:NT], op=ALU.mult),
                 reads=[("ps", bk2), ("gBm", c)], writes=[("hid", 16 + c)])
            if pend is not None:
                stats(*pend)
            pend = (c, sq_slot)
        stats(*pend)
        return bM, bE

    def stage_ln(t, bM, bE):
        NT = t["NT"]
        cabv, cabk = cab_of(t)
        S.op("act", lambda e: e.activation(out=dummy[:, 1:2], in_=dummy[:, 0:1], func=AF.Ln), writes=["dummy"])
        S.op("act", lambda e: e.activation(out=tf[:, 0, 0:NT], in_=ps[:, bM, 0:NT], func=AF.Square),
             reads=[("ps", bM)], writes=[("tf", 0)])
        S.op("dve", lambda e: e.tensor_tensor(out=tf[:, 1, 0:NT], in0=ps[:, bE, 0:NT], in1=tf[:, 0, 0:NT],
                                             op=ALU.subtract),
             reads=[("ps", bE), ("tf", 0)], writes=[("tf", 1)])
        S.op("act", lambda e: e.activation(out=tf[:, 0, 0:NT], in_=tf[:, 1, 0:NT], func=AF.Ln, bias=EPS),
             reads=[("tf", 1)], writes=[("tf", 0)])
        S.op("act", lambda e: e.activation(out=tf[:, 1, 0:NT], in_=tf[:, 0, 0:NT], func=AF.Exp, scale=-0.5),
             reads=[("tf", 0)], writes=[("tf", 1)])
        S.op("act", lambda e: e.activation(out=dummy[:, 1:2], in_=dummy[:, 0:1], func=AF.Silu), writes=["dummy"])
        S.op("dve", lambda e: e.scalar_tensor_tensor(out=tf[:, 2, 0:NT], in0=ps[:, bM, 0:NT], scalar=-1.0,
                                                    in1=tf[:, 1, 0:NT], op0=ALU.mult, op1=ALU.mult),
             reads=[("ps", bM), ("tf", 1)], writes=[("tf", 2)])
        unhold(bM)
        unhold(bE)
        if t.get("convb_later"):
            convb_block(t)
        for c in range(KC):
            sl_i = 3 + c % 3
            S.op("dve", lambda e, c=c, sl_i=sl_i: e.tensor_tensor(out=tf[:, sl_i, 0:NT], in0=cabv(c)[:, 0:NT],
                                                                 in1=tf[:, 1, 0:NT], op=ALU.mult),
                 reads=[cabk(c), ("tf", 1)], writes=[("tf", sl_i)])
            S.op("pool" if c % 2 == 0 else "dve", lambda e, sl_i=sl_i: e.tensor_tensor(
                out=tf[:, sl_i, 0:NT], in0=tf[:, sl_i, 0:NT], in1=tf[:, 2, 0:NT], op=ALU.add),
                 reads=[("tf", sl_i), ("tf", 2)], writes=[("tf", sl_i)])
            S.op("act", lambda e, c=c, sl_i=sl_i: e.activation(
                out=hid[:, 8 + c, 0:NT], in_=tf[:, sl_i, 0:NT], func=AF.Silu,
                scale=cvec[:, c, R_LNG:R_LNG + 1], bias=cvec[:, c, R_LNB:R_LNB + 1]),
                reads=[("tf", sl_i), "cvec"], writes=[("hid", 8 + c)])

    def stage_wb_wa(t):
        NT = t["NT"]
        for which in ("wb", "wa"):
            base = 16 if which == "wb" else 8
            for h in range(2):
                wi, slot, wv = get_w(which)
                for i in range(4):
                    c = h * 4 + i
                    bk = next_bank()

                    def mm(e, i=i, bk=bk, wv=wv, base=base):
                        ins = None
                        for kc in range(KC):
                            ins = e.matmul(ps[:, bk, 0:NT], lhsT=wv[:, kc, i * P:(i + 1) * P],
                                           rhs=hid[:, base + kc, 0:NT], start=(kc == 0), stop=(kc == KC - 1))
                        return ins

                    S.op("pe", mm, reads=[("ring", slot)] + [("hid", base + k) for k in range(KC)],
                         writes=[("ps", bk)])
                    if which == "wb":
                        S.op("dve", lambda e, c=c, bk=bk: e.tensor_tensor(out=hid[:, 24 + c, 0:NT],
                                                                         in0=ps[:, bk, 0:NT], in1=gb[:, c, 0:NT],
                                                                         op=ALU.mult),
                             reads=[("ps", bk), ("gb", c)], writes=[("hid", 24 + c)])
                    else:
                        ms = c % 4
                        S.op("dve", lambda e, c=c, bk=bk, ms=ms: e.tensor_tensor(out=tb[:, ms, 0:NT],
                                                                                in0=ps[:, bk, 0:NT],
                                                                                in1=ga[:, c, 0:NT], op=ALU.mult),
                             reads=[("ps", bk), ("ga", c)], writes=[("tb", ms)])
                        S.op("pool", lambda e, c=c, ms=ms: e.tensor_tensor(out=gBm[:, c, 0:NT], in0=tb[:, ms, 0:NT],
                                                                          in1=hid[:, 24 + c, 0:NT], op=ALU.add),
                             reads=[("tb", ms), ("hid", 24 + c)], writes=[("gBm", c)])
                release_w(wi)

    def stage_wo(t):
        buf, ns = t["buf"], t["nsub"]
        for j in range(2):
            wi, slot, wv = get_w("wo")
            for s in range(ns):
                bk = next_bank()

                def mm(e, s=s, bk=bk, wv=wv):
                    ins = None
                    for kc in range(KC):
                        ins = e.matmul(ps[:, bk, :], lhsT=gBm[:, kc, s * P:(s + 1) * P], rhs=wv[:, kc, :],
                                       start=(kc == 0), stop=(kc == KC - 1))
                    return ins

                S.op("pe", mm, reads=[("ring", slot)] + [("gBm", k) for k in range(KC)], writes=[("ps", bk)])
                S.op("dve", lambda e, s=s, j=j, bk=bk: e.tensor_tensor(
                    out=xt[:, buf, s, j * 512:(j + 1) * 512], in0=ps[:, bk, :],
                    in1=xt[:, buf, s, j * 512:(j + 1) * 512], op=ALU.add),
                    reads=[("ps", bk), ("xt", buf, s)], writes=[("xt", buf, s)])
            release_w(wi)

    def stage_wup(t):
        NT = t["NT"]
        rr = 0
        for g in range(8):
            wi, slot, wv = get_w("wup")
            for i in range(4):
                f = g * 4 + i
                bk = next_bank()

                def mm(e, i=i, bk=bk, wv=wv):
                    ins = None
                    for kc in range(KC):
                        ins = e.matmul(ps[:, bk, 0:NT], lhsT=wv[:, kc, i * P:(i + 1) * P], rhs=h2T[:, kc, 0:NT],
                                       start=(kc == 0), stop=(kc == KC - 1))
                    return ins

                S.op("pe", mm, reads=[("ring", slot)] + [("h2T", k) for k in range(KC)], writes=[("ps", bk)])
                S.op("act", lambda e, bk=bk, f=f: e.activation(out=hid[:, f, 0:NT], in_=ps[:, bk, 0:NT],
                                                              func=AF.Relu),
                     reads=[("ps", bk)], writes=[("hid", f)])
                S.op("act", lambda e, f=f: e.activation(out=hid[:, f, 0:NT], in_=hid[:, f, 0:NT], func=AF.Square),
                     reads=[("hid", f)], writes=[("hid", f)])
                pump(2)
            release_w(wi)

    def stage_wdown(t):
        buf, ns = t["buf"], t["nsub"]
        for j in range(2):
            banks = [next_bank(hold=True) for _ in range(ns)]
            for q in range(4):
                wi, slot, wv = get_w("wdn")
                for s in range(ns):
                    bk = banks[s]

                    def mm(e, s=s, bk=bk, wv=wv, q=q):
                        ins = None
                        for fk in range(8):
                            ins = e.matmul(ps[:, bk, :], lhsT=hid[:, q * 8 + fk, s * P:(s + 1) * P], rhs=wv[:, fk, :],
                                           start=(q == 0 and fk == 0), stop=(q == 3 and fk == 7))
                        return ins

                    S.op("pe", mm, reads=[("ring", slot)] + [("hid", q * 8 + fk) for fk in range(8)],
                         writes=[("ps", bk)])
                release_w(wi)
                pump(10)
            for s in range(ns):
                bk = banks[s]
                S.op("dve", lambda e, s=s, j=j, bk=bk: e.tensor_tensor(
                    out=xt[:, buf, s, j * 512:(j + 1) * 512], in0=ps[:, bk, :],
                    in1=xt[:, buf, s, j * 512:(j + 1) * 512], op=ALU.add),
                    reads=[("ps", bk), ("xt", buf, s)], writes=[("xt", buf, s)])
                unhold(bk)

    def stage_final(t):
        buf, ns, t0, NT = t["buf"], t["nsub"], t["tok0"], t["NT"]
        stage = 2
        for s in range(ns):
            ssap = stat[:, stage, 0, s:s + 1]
            S.op("act", lambda e, s=s, ssap=ssap: e.activation(out=xn[:, s, :], in_=xt[:, buf, s, :], func=AF.Square,
                                                              accum_out=ssap),
                 reads=[("xt", buf, s)], writes=[("st", stage, 0, s), ("xn", s)])
            rsap = rstd_ops(stage, s)
            S.op("dve", lambda e, s=s, rsap=rsap: e.scalar_tensor_tensor(
                out=xt[:, buf, s, :], in0=xt[:, buf, s, :], scalar=rsap, in1=gfin[:], op0=ALU.mult, op1=ALU.mult),
                reads=[("xt", buf, s), ("st", stage, 2, s), "gfin"], writes=[("xt", buf, s)])
        dst = yout.ap()[t0:t0 + NT, :].rearrange("(s p) d -> p s d", p=P)
        S.op("pool", lambda e, dst=dst: e.dma_start(out=dst, in_=xt[:, buf, 0:ns, :]),
             reads=[("xt", buf, s) for s in range(ns)], dma_sem="s_y%d" % buf)

    t0_ = tiles[0]
    rms_part(t0_, 0)
    stage_hist(t0_)
    casts_win()
    casts_attn()
    if KPE0:
        build_diag_a()
    for i in range(min(NSLOT, len(wseq))):
        record_wload(i)
    tr_part(t0_, R_G1, hT, "hT")
    stage_win(t0_, 0, len(WIN_GROUPS))
    GSPLIT = 12
    for t in tiles:
        nxt = tiles[t["idx"] + 1] if t["idx"] + 1 < len(tiles) else None
        if t["last"]:
            stage_cache_out(t)
        bM, bE = stage_conv(t)
        stage_ln(t, bM, bE)
        if t["idx"] == 0:
            casts_up()
        if nxt is not None:
            rms_part(nxt, 0)
        stage_wb_wa(t)
        if nxt is not None:
            tr_part(nxt, R_G1, hT, "hT")
        stage_wo(t)
        if t["idx"] == 0:
            casts_down()
        rms_part(t, 1)
        if nxt is not None:
            stage_hist(nxt)
            stage_win(nxt, 0, GSPLIT)
        tr_part(t, R_G2, h2T, "h2T")
        if nxt is not None:
            stage_win(nxt, GSPLIT, len(WIN_GROUPS))
        stage_wup(t)
        stage_wdown(t)
        if nxt is not None and nxt["kpe"] == 0:
            pre_squares(nxt)
        stage_final(t)
        if t["idx"] + 2 < len(tiles):
            load_x(tiles[t["idx"] + 2])

    S.wait_all("pool", [s for s in sorted(S.sem_names) if s.startswith(("s_y", "s_oc", "s_dbg"))])

    sems = {name: es.enter_context(nc.semaphore(name)) for name in sorted(S.sem_names)}

    def run(stream, e):
        for item in stream:
            if item[0] == "wait":
                e.wait_ge(sems[item[1]], item[2])
            else:
                _, fn, sem, inc = item
                ins = fn(e)
                ins.then_inc(sems[sem], inc)

    with nc.Block() as block:
        @block.tensor
        def _(e):
            run(S.streams["pe"], e)

        @block.scalar
        def _(e):
            run(S.streams["act"], e)

        @block.vector
        def _(e):
            run(S.streams["dve"], e)

        @block.gpsimd
        def _(e):
            run(S.streams["pool"], e)

        @block.sync
        def _(e):
            run(S.streams["sp"], e)
    es.close()
    return nc


def pack_vecs(norm1_g, dw_a, dw_a_b, ln_a_g, ln_a_b, dw_b, gate_bias, norm2_g):
    v = np.zeros((NV, D), np.float32)
    v[R_G1] = norm1_g[0]
    v[R_DWA:R_DWA + WA] = dw_a[0]
    v[R_DWAB] = dw_a_b[0]
    v[R_LNG] = ln_a_g[0]
    v[R_LNB] = ln_a_b[0]
    v[R_DWB:R_DWB + WB] = dw_b[0]
    v[R_GBA] = gate_bias[0][:D]
    v[R_GBB] = gate_bias[0][D:]
    v[R_G2] = norm2_g[0]
    return v


_NC_CACHE = {}


def run_cores(NP, PL, NS, n_cores, x_prompt, x_sample, cache_conv_a, cache_conv_b, norm1_g, w_in, dw_a, dw_a_b,
              ln_a_g, ln_a_b, wa_out, dw_b, wb_out, gate_bias, w_o, norm2_g, w_up, w_down, final_norm_g,
              debug=()):
    f = lambda a: np.ascontiguousarray(np.asarray(a, dtype=np.float32))
    x_prompt, x_sample, cache_conv_a, cache_conv_b = f(x_prompt), f(x_sample), f(cache_conv_a), f(cache_conv_b)
    key = (NP, PL, NS, tuple(debug))
    if key not in _NC_CACHE:
        _NC_CACHE[key] = build(NP, PL, NS, debug)
    nc = _NC_CACHE[key]
    vecs = pack_vecs(f(norm1_g), f(dw_a), f(dw_a_b), f(ln_a_g), f(ln_a_b), f(dw_b), f(gate_bias), f(norm2_g))
    shared = dict(vecs=vecs, gfin=f(final_norm_g).reshape(1, D), ident=np.eye(P, dtype=np.float32),
                  w_in=f(w_in)[0], wa_out=f(wa_out)[0], wb_out=f(wb_out)[0], w_o=f(w_o)[0], w_up=f(w_up)[0],
                  w_down=f(w_down)[0])
    in_maps = []
    for c in range(n_cores):
        xp = x_prompt[c * NP:(c + 1) * NP].reshape(NP * PL, D)
        xs = x_sample[c * NS:(c + 1) * NS].reshape(NS * SL, D)
        m = dict(shared)
        m["xin"] = np.ascontiguousarray(np.concatenate([xp, xs], axis=0))
        m["hca"] = np.ascontiguousarray(cache_conv_a[0, c * NS:(c + 1) * NS].reshape(NS * HA, D))
        m["hcb"] = np.ascontiguousarray(cache_conv_b[0, c * NS:(c + 1) * NS].reshape(NS * HB, D))
        in_maps.append(m)
    res = run_bass_kernel_spmd(nc, in_maps, core_ids=list(range(n_cores)))
    yp, ys, pa, pb, sa, sbb = [], [], [], [], [], []
    for c in range(n_cores):
        r = res.results[c]
        y = np.asarray(r["yout"], dtype=np.float32)
        yp.append(y[:NP * PL].reshape(NP, PL, D))
        ys.append(y[NP * PL:].reshape(NS, SL, D))
        a = np.asarray(r["oca"], dtype=np.float32).reshape(NP + NS, HA, D)
        b = np.asarray(r["ocb"], dtype=np.float32).reshape(NP + NS, HB, D)
        pa.append(a[:NP])
        sa.append(a[NP:])
        pb.append(b[:NP])
        sbb.append(b[NP:])
    out = (np.concatenate(yp, 0), np.concatenate(ys, 0), np.concatenate(pa, 0)[None], np.concatenate(pb, 0)[None],
           np.concatenate(sa, 0)[None], np.concatenate(sbb, 0)[None])
    return out, res


def kernel(**inputs):
    out, _ = run_cores(2, 4096, 4, 8, **inputs)
    return out
```

```python
import numpy as np
import concourse.bass as bass
import concourse.mybir as mybir
from concourse.bass_utils import run_bass_kernel_spmd

F32 = mybir.dt.float32
BF16 = mybir.dt.bfloat16
AF = mybir.ActivationFunctionType
ALU = mybir.AluOpType

P = 128
D = 1024
KC = 8
NIN = 7168
DFF = 4096
FC = 32
HA = 30
HB = 2
WA = 31
WB = 3
T = 512
SL = 64
UW = 544
CW = 520
EPS = 1e-6
NV = 42
R_G1, R_DWA, R_DWAB, R_LNG, R_LNB, R_DWB, R_GBA, R_GBB, R_G2 = 0, 1, 32, 33, 34, 35, 38, 39, 40
NSLOT = 4
KPE0 = 16
SEC_OFF = {"val": 0, "gate": 1024, "gB": 2048, "gC": 3072, "hB": 4096, "ga": 5120, "gb": 6144}
WIN_GROUPS = [("gate", 0), ("val", 0), ("gate", 1), ("val", 1), ("gC", 0), ("hB", 0), ("gC", 1), ("hB", 1),
              ("gB", 0), ("gB", 1), ("ga", 0), ("ga", 1), ("gb", 0), ("gb", 1)]


class Sched:
    def __init__(self):
        self.streams = {e: [] for e in ("pe", "act", "dve", "pool", "sp")}
        self.cnt = {}
        self.waited = {e: {} for e in self.streams}
        self.last_w = {}
        self.readers = {}
        self.eng_sem = {"pe": "s_pe", "act": "s_act", "dve": "s_dve", "pool": "s_pool"}
        self.sem_names = set(self.eng_sem.values())

    def op(self, eng, fn, reads=(), writes=(), dma_sem=None):
        deps = {}

        def add(tok):
            if tok is None:
                return
            s, v = tok
            if deps.get(s, 0) < v:
                deps[s] = v

        for k in reads:
            add(self.last_w.get(k))
        for k in writes:
            add(self.last_w.get(k))
            for t in self.readers.get(k, ()):
                add(t)
        w = self.waited[eng]
        if eng == "pe":
            deps.pop("s_pe", None)
        for s, v in deps.items():
            if w.get(s, 0) < v:
                w[s] = v
                self.streams[eng].append(("wait", s, v))
        if dma_sem is not None:
            sem, inc = dma_sem, 16
        else:
            sem, inc = self.eng_sem[eng], 1
        self.sem_names.add(sem)
        self.cnt[sem] = self.cnt.get(sem, 0) + inc
        tok = (sem, self.cnt[sem])
        self.streams[eng].append(("op", fn, sem, inc))
        for k in writes:
            self.last_w[k] = tok
            self.readers[k] = []
        for k in reads:
            self.readers.setdefault(k, []).append(tok)
        return tok

    def wait_all(self, eng, sems):
        for s in sems:
            v = self.cnt.get(s, 0)
            if v and self.waited[eng].get(s, 0) < v:
                self.waited[eng][s] = v
                self.streams[eng].append(("wait", s, v))


def make_tiles(NP, PL, NS):
    tiles = []
    tok = 0
    for b in range(NP):
        nt = PL // T
        for j in range(nt):
            tiles.append(dict(kind="prompt", seq=b, first=(j == 0), last=(j == nt - 1), tok0=tok,
                              NT=T, nsub=T // P, nseg=1, seglen=T))
            tok += T
    if NS:
        tiles.append(dict(kind="samp", seq=NP, first=True, last=True, tok0=NP * PL,
                          NT=NS * SL, nsub=NS * SL // P, nseg=NS, seglen=SL))
    for i, t in enumerate(tiles):
        t["idx"] = i
        t["buf"] = i % 2
        t["kpe"] = KPE0 if i == 0 else 0
    return tiles


def build(NP, PL, NS, debug=()):
    assert PL % T == 0 and (NS * SL) % P == 0
    nc = bass.Bass("TRN2", target_bir_lowering=False)
    NTOK = NP * PL + NS * SL
    NSEQ = NP + NS
    dt_in = lambda name, shape: nc.dram_tensor(name, shape, F32, kind="ExternalInput")
    xin = dt_in("xin", [NTOK, D])
    hca = dt_in("hca", [max(NS, 1) * HA, D])
    hcb = dt_in("hcb", [max(NS, 1) * HB, D])
    vecs = dt_in("vecs", [NV, D])
    gfin_d = dt_in("gfin", [1, D])
    ident_d = dt_in("ident", [P, P])
    w_in_d = dt_in("w_in", [D, NIN])
    wa_d = dt_in("wa_out", [D, D])
    wb_d = dt_in("wb_out", [D, D])
    wo_d = dt_in("w_o", [D, D])
    wup_d = dt_in("w_up", [D, DFF])
    wdn_d = dt_in("w_down", [DFF, D])
    yout = nc.dram_tensor("yout", [NTOK, D], F32, kind="ExternalOutput")
    oca = nc.dram_tensor("oca", [NSEQ * HA, D], F32, kind="ExternalOutput")
    ocb = nc.dram_tensor("ocb", [NSEQ * HB, D], F32, kind="ExternalOutput")
    sc = lambda name, shape: nc.dram_tensor(name, shape, BF16, kind="Internal")
    w_in_s = sc("w_in_s", [D, NIN])
    wa_s = sc("wa_s", [D, D])
    wb_s = sc("wb_s", [D, D])
    wo_s = sc("wo_s", [D, D])
    wup_s = sc("wup_s", [D, DFF])
    wdn_s = sc("wdn_s", [DFF, D])
    dga_s = sc("dga_s", [KC, P, WA * P])
    dbg_out = {}
    for name, shape in debug:
        dbg_out[name] = nc.dram_tensor("dbg_" + name, list(shape), F32, kind="ExternalOutput")

    tiles = make_tiles(NP, PL, NS)
    S = Sched()

    import contextlib
    es = contextlib.ExitStack()
    sb = lambda name, shape, dt: es.enter_context(nc.sbuf_tensor(name, shape, dt))
    ident_f = sb("ident_f", [P, P], F32)
    ident_b = sb("ident_b", [P, P], BF16)
    ones_b = sb("ones_b", [P, P], BF16)
    cvec = sb("cvec", [P, KC, NV], F32)
    gfin = sb("gfin_sb", [P, D], F32)
    dgb = sb("dgb", [P, KC, WB, P], BF16)
    xt = sb("xt", [P, 2, T // P, D], F32)
    xn = sb("xn", [P, T // P, D], BF16)
    hT = sb("hT", [P, KC, T], BF16)
    h2T = sb("h2T", [P, KC, T], BF16)
    ub = sb("ub", [P, KC, UW], BF16)
    cbn = sb("cbn", [P, KC, CW], BF16)
    gBm = sb("gBm", [P, KC, T], BF16)
    ga = sb("ga", [P, KC, T], BF16)
    gb = sb("gb", [P, KC, T], BF16)
    hid = sb("hid", [P, FC, T], BF16)
    tf = sb("tf", [P, 6, T], F32)
    tb = sb("tb", [P, 6, T], BF16)
    junk = tb[:, 4:6, :].rearrange("p a b -> p (a b)")
    acc = sb("acc", [P, 2, T], F32)
    accb = sb("accb", [P, KC, T], BF16)
    mhalf = sb("mhalf", [P, 4], F32)
    dummy = sb("dummy_act", [P, 2], F32)
    stat = sb("stat", [P, 3, 3, 4], F32)
    hco = sb("hco", [P, D], F32)
    uo = tf[:, 4:6, :].rearrange("p a (c w) -> p (a c) w", w=P)
    ring = sb("ring", [P, NSLOT, 4096], BF16)
    ps = es.enter_context(nc.psum_tensor("ps", [P, 8, 512], F32))
    psb = ps.bitcast(BF16)

    bank_state = {"next": 0, "held": set()}

    def next_bank(hold=False):
        while True:
            b = bank_state["next"]
            bank_state["next"] = (b + 1) % 8
            if b not in bank_state["held"]:
                break
        if hold:
            bank_state["held"].add(b)
        return b

    def unhold(b):
        bank_state["held"].discard(b)

    def wuses_for_tile():
        u = []
        for g, (sec, h) in enumerate(WIN_GROUPS):
            c0 = SEC_OFF[sec] + h * 512
            u.append((("w_in", g), w_in_s.ap()[:, c0:c0 + 512].rearrange("(k p) n -> p k n", p=P), "k"))
        for c in range(KC if KPE0 else 0):
            u.append((("dga", c), dga_s.ap()[c][:, 0:KPE0 * P], "flat"))
        for h in range(2):
            u.append((("wb", h), wb_s.ap()[:, h * 512:(h + 1) * 512].rearrange("(k p) n -> p k n", p=P), "k"))
        for h in range(2):
            u.append((("wa", h), wa_s.ap()[:, h * 512:(h + 1) * 512].rearrange("(k p) n -> p k n", p=P), "k"))
        for j in range(2):
            u.append((("wo", j), wo_s.ap()[:, j * 512:(j + 1) * 512].rearrange("(k p) n -> p k n", p=P), "k"))
        for g in range(8):
            u.append((("wup", g), wup_s.ap()[:, g * 512:(g + 1) * 512].rearrange("(k p) n -> p k n", p=P), "k"))
        for j in range(2):
            for q in range(4):
                u.append((("wdn", j, q),
                          wdn_s.ap()[q * 1024:(q + 1) * 1024, j * 512:(j + 1) * 512].rearrange("(k p) n -> p k n", p=P),
                          "k"))
        return u

    per_tile = wuses_for_tile()
    sect = {}
    for u in per_tile:
        sect.setdefault(u[0][0], []).append(u)
    wseq = list(sect["w_in"])
    for ti in range(len(tiles)):
        wseq += (sect.get("dga", []) if ti == 0 else []) + sect["wb"] + sect["wa"] + sect["wo"]
        if ti + 1 < len(tiles):
            wseq += sect["w_in"]
        wseq += sect["wup"] + sect["wdn"]
    wstate = {"next": 0}

    def ring_view(slot, kind):
        if kind == "k":
            return ring[:, slot, :].rearrange("p (k n) -> p k n", n=512)
        return ring[:, slot, 0:KPE0 * P]

    def record_wload(i):
        key, src, kind = wseq[i]
        slot = i % NSLOT
        dst = ring_view(slot, kind)
        assert ("wsc",) + key in S.last_w, key
        S.op("sp", lambda e, dst=dst, src=src: e.dma_start(out=dst, in_=src),
             reads=[("wsc",) + key], writes=[("ring", slot)], dma_sem="s_w%d" % slot)

    def get_w(expect):
        i = wstate["next"]
        wstate["next"] += 1
        key, src, kind = wseq[i]
        assert key[0] == expect, (key, expect)
        slot = i % NSLOT
        return i, slot, ring_view(slot, kind)

    def release_w(i):
        if i + NSLOT < len(wseq):
            record_wload(i + NSLOT)

    def dump(name, src_ap, keys):
        if name in dbg_out:
            S.op("pool", lambda e, o=dbg_out[name], s=src_ap: e.dma_start(out=o.ap(), in_=s),
                 reads=keys, dma_sem="s_dbg")

    S.op("sp", lambda e: e.dma_start(out=ident_f[:], in_=ident_d.ap()), writes=["ident_f"], dma_sem="s_const")
    S.op("sp", lambda e: e.dma_start(out=hco[0:NV, :], in_=vecs.ap()), writes=["hco"], dma_sem="s_const")
    S.op("sp", lambda e: e.dma_start(out=gfin[:], in_=bass.AP(gfin_d, 0, [[0, P], [1, D]])),
         writes=["gfin"], dma_sem="s_const")
    ctok = ("s_const", S.cnt["s_const"])
    for k in ("ident_f", "hco", "gfin"):
        S.last_w[k] = ctok

    def load_x(t):
        buf, ns, t0 = t["buf"], t["nsub"], t["tok0"]
        src = xin.ap()[t0:t0 + t["NT"], :].rearrange("(s p) d -> p s d", p=P)
        S.op("pool", lambda e, src=src, buf=buf, ns=ns: e.dma_start(out=xt[:, buf, 0:ns, :], in_=src),
             writes=[("xt", buf, s) for s in range(ns)], dma_sem="s_x%d" % buf)

    load_x(tiles[0])
    if len(tiles) > 1:
        load_x(tiles[1])

    S.op("pool", lambda e: e.memset(mhalf[:], -0.5), writes=["mhalf"])
    S.op("pool", lambda e: e.memset(dummy[:], 1.0), writes=["dummy"])

    def cast(key, dst, src):
        S.op("pool", lambda e, dst=dst, src=src: e.dma_start(out=dst, in_=src),
             writes=[("wsc",) + key], dma_sem="s_c_" + "_".join(str(x) for x in key))

    def casts_win():
        for g, (sec, h) in enumerate(WIN_GROUPS):
            c0 = SEC_OFF[sec] + h * 512
            cast(("w_in", g), w_in_s.ap()[:, c0:c0 + 512], w_in_d.ap()[:, c0:c0 + 512])

    def casts_attn():
        for h in range(2):
            cast(("wb", h), wb_s.ap()[:, h * 512:(h + 1) * 512], wb_d.ap()[:, h * 512:(h + 1) * 512])
        for h in range(2):
            cast(("wa", h), wa_s.ap()[:, h * 512:(h + 1) * 512], wa_d.ap()[:, h * 512:(h + 1) * 512])
        for j in range(2):
            cast(("wo", j), wo_s.ap()[:, j * 512:(j + 1) * 512], wo_d.ap()[:, j * 512:(j + 1) * 512])

    def casts_up():
        for g in range(8):
            cast(("wup", g), wup_s.ap()[:, g * 512:(g + 1) * 512], wup_d.ap()[:, g * 512:(g + 1) * 512])

    def casts_down():
        for j in range(2):
            for q in range(4):
                cast(("wdn", j, q), wdn_s.ap()[q * 1024:(q + 1) * 1024, j * 512:(j + 1) * 512],
                     wdn_d.ap()[q * 1024:(q + 1) * 1024, j * 512:(j + 1) * 512])

    S.op("dve", lambda e: e.tensor_copy(out=ident_b[:], in_=ident_f[:]), reads=["ident_f"], writes=["ident_b"])
    S.op("dve", lambda e: e.memset(ones_b[:], 1.0 / D), writes=["ones_b"])

    bk = next_bank()

    def tr_vecs(e, bk=bk):
        ins = None
        for c in range(KC):
            ins = e.transpose(out=ps[:, bk, c * 64:c * 64 + NV], in_=hco[0:NV, c * P:(c + 1) * P],
                              identity=ident_f[0:NV, 0:NV])
        return ins

    S.op("pe", tr_vecs, reads=["hco", "ident_f"], writes=[("ps", bk)])
    S.op("dve", lambda e, bk=bk: e.tensor_copy(
        out=cvec[:], in_=ps[:, bk, :].rearrange("p (c r) -> p c r", r=64)[:, :, 0:NV]),
        reads=[("ps", bk)], writes=["cvec"])

    def build_dgb(e):
        ins = None
        for c in range(KC):
            for k in range(WB):
                ins = e.tensor_scalar(out=dgb[:, c, k, :], in0=ident_f[:], scalar1=cvec[:, c, R_DWB + k:R_DWB + k + 1],
                                      scalar2=None, op0=ALU.mult)
        return ins

    S.op("dve", build_dgb, reads=["cvec", "ident_f"], writes=["dgb"])

    def build_diag_a():
        for c in range(KC):
            st = c % 4
            stg = hid[:, st * 8:(st + 1) * 8, :].rearrange("p a b -> p (a b)")[:, 0:KPE0 * P]
            keys = [("hid", f) for f in range(st * 8, st * 8 + 8)]

            def build_dga(e, c=c, stg=stg):
                ins = None
                for k in range(KPE0):
                    ins = e.tensor_scalar(out=stg[:, k * P:(k + 1) * P], in0=ident_f[:],
                                          scalar1=cvec[:, c, R_DWA + k:R_DWA + k + 1], scalar2=None, op0=ALU.mult)
                return ins

            S.op("dve", build_dga, reads=["cvec", "ident_f"], writes=keys)
            S.op("sp", lambda e, c=c, stg=stg: e.dma_start(out=dga_s.ap()[c][:, 0:KPE0 * P], in_=stg),
                 reads=keys, writes=[("wsc", "dga", c)], dma_sem="s_dg%d" % c)

    def rstd_ops(stage, s):
        ssap = stat[:, stage, 0, s:s + 1]
        sdap = stat[:, stage, 1, s:s + 1]
        rsap = stat[:, stage, 2, s:s + 1]
        S.op("pool", lambda e: e.tensor_scalar(out=sdap, in0=ssap, scalar1=1.0 / D, scalar2=EPS, op0=ALU.mult,
                                               op1=ALU.add),
             reads=[("st", stage, 0, s)], writes=[("st", stage, 1, s)])
        S.op("pool", lambda e: e.tensor_tensor(out=rsap, in0=sdap, in1=mhalf[:, 0:1], op=ALU.pow),
             reads=[("st", stage, 1, s), "mhalf"], writes=[("st", stage, 2, s)])
        return rsap

    def rms_part(t, stage):
        buf, ns = t["buf"], t["nsub"]
        rs = []
        for s in range(ns):
            ssap = stat[:, stage, 0, s:s + 1]
            S.op("act", lambda e, s=s, ssap=ssap: e.activation(out=xn[:, s, :], in_=xt[:, buf, s, :], func=AF.Square,
                                                              accum_out=ssap),
                 reads=[("xt", buf, s)], writes=[("st", stage, 0, s), ("xn", s)])
            rs.append(rstd_ops(stage, s))
        for s in range(ns):
            S.op("act", lambda e, s=s, rsap=rs[s]: e.activation(out=xn[:, s, :], in_=xt[:, buf, s, :], func=AF.Copy,
                                                               scale=rsap),
                 reads=[("xt", buf, s), ("st", stage, 2, s)], writes=[("xn", s)])

    def tr_part(t, grow, dst, dkey):
        ns, NT = t["nsub"], t["NT"]
        for q in range(KC // 2):
            bk = next_bank()

            def trs(e, q=q, bk=bk):
                ins = None
                for kk in range(2):
                    k = 2 * q + kk
                    for s in range(ns):
                        ins = e.transpose(out=psb[:, bk, kk * 512 + s * P:kk * 512 + (s + 1) * P],
                                          in_=xn[:, s, k * P:(k + 1) * P], identity=ident_b[:])
                return ins

            S.op("pe", trs, reads=[("xn", s) for s in range(ns)] + ["ident_b"], writes=[("ps", bk)])
            for kk in range(2):
                k = 2 * q + kk
                S.op("act", lambda e, k=k, kk=kk, bk=bk: e.activation(
                    out=dst[:, k, 0:NT], in_=psb[:, bk, kk * 512:kk * 512 + NT], func=AF.Copy,
                    scale=cvec[:, k, grow:grow + 1]),
                    reads=[("ps", bk), "cvec"], writes=[(dkey, k)])

    def uview(t, c, off, width):
        if t["nseg"] == 1:
            return ub[:, c, off:off + width]
        L = HA + t["seglen"]
        return ub[:, c, 0:t["nseg"] * L].rearrange("p (s j) -> p s j", j=L)[:, :, off:off + width]

    def cview(t, c, off, width):
        if t["nseg"] == 1:
            return cbn[:, c, off:off + width]
        L = HB + t["seglen"]
        return cbn[:, c, 0:t["nseg"] * L].rearrange("p (s j) -> p s j", j=L)[:, :, off:off + width]

    def pview(t, ap2d):
        if t["nseg"] == 1:
            return ap2d
        return ap2d.rearrange("p (s j) -> p s j", j=t["seglen"])

    def tail(t, ap2d, n):
        sl = t["seglen"]
        if t["nseg"] == 1:
            return ap2d[:, sl - n:sl]
        return ap2d.rearrange("p (s j) -> p s j", j=sl)[:, :, sl - n:sl]

    conv_q = []

    def queue_conv(t, pairs):
        sl, NT = t["seglen"], t["NT"]
        kpe = t["kpe"]
        taps = list(range(kpe, WA))
        for pair in pairs:
            for ti, k in enumerate(taps):
                for a in range(2):
                    c = 2 * pair + a
                    accv = pview(t, acc[:, a, 0:NT])
                    wk = cvec[:, c, R_DWA + k:R_DWA + k + 1]
                    if ti == 0 and kpe == 0:
                        fn = lambda e, c=c, k=k, accv=accv, wk=wk: e.tensor_scalar(
                            out=accv, in0=uview(t, c, k, sl), scalar1=wk, scalar2=cvec[:, c, R_DWAB:R_DWAB + 1],
                            op0=ALU.mult, op1=ALU.add)
                        conv_q.append((fn, [("ub", c), "cvec"], [("acc", a)]))
                    elif ti == 0:
                        fn = lambda e, c=c, k=k, accv=accv, wk=wk: e.tensor_scalar(
                            out=accv, in0=uview(t, c, k, sl), scalar1=wk, scalar2=None, op0=ALU.mult)
                        conv_q.append((fn, [("ub", c), "cvec"], [("acc", a)]))
                    elif ti < len(taps) - 1:
                        fn = lambda e, c=c, k=k, accv=accv, wk=wk: e.scalar_tensor_tensor(
                            out=accv, in0=uview(t, c, k, sl), scalar=wk, in1=accv, op0=ALU.mult, op1=ALU.add)
                        conv_q.append((fn, [("ub", c), ("acc", a), "cvec"], [("acc", a)]))
                    else:
                        fn = lambda e, c=c, k=k, accv=accv, wk=wk: e.scalar_tensor_tensor(
                            out=pview(t, accb[:, c, 0:NT]), in0=uview(t, c, k, sl), scalar=wk, in1=accv,
                            op0=ALU.mult, op1=ALU.add)
                        conv_q.append((fn, [("ub", c), ("acc", a), "cvec"], [("accb", c)]))

    def pump(n):
        for _ in range(min(n, len(conv_q))):
            fn, rd, wr = conv_q.pop(0)
            S.op("dve", fn, reads=rd, writes=wr)

    def stage_hist(t):
        ukeys = [("ub", c) for c in range(KC)]
        ckeys = [("cbn", c) for c in range(KC)]
        if t["kind"] == "prompt":
            if t["first"]:
                S.op("pool", lambda e: e.memset(ub[:, :, 0:HA], 0.0), writes=ukeys)
                S.op("pool", lambda e: e.memset(cbn[:, :, 0:HB], 0.0), writes=ckeys)
            else:
                S.op("pool", lambda e: e.tensor_copy(out=ub[:, :, 0:HA], in_=ub[:, :, T:T + HA]),
                     reads=ukeys, writes=ukeys)
                S.op("pool", lambda e: e.tensor_copy(out=cbn[:, :, 0:HB], in_=cbn[:, :, T:T + HB]),
                     reads=ckeys, writes=ckeys)
        else:
            ns = t["nseg"]
            S.op("pool", lambda e: e.dma_start(out=hco[0:ns * HA, :], in_=hca.ap()), writes=["hco"], dma_sem="s_hc")
            S.op("pool", lambda e: e.dma_start(out=hco[P - ns * HB:P, :], in_=hcb.ap()), writes=["hco2"],
                 dma_sem="s_hc")
            tok = ("s_hc", S.cnt["s_hc"])
            S.last_w["hco"] = tok
            S.last_w["hco2"] = tok
            for half in range(2):
                bk = next_bank()

                def trh(e, half=half, bk=bk):
                    ins = None
                    for cc in range(4):
                        c = half * 4 + cc
                        ins = e.transpose(out=ps[:, bk, cc * P:(cc + 1) * P], in_=hco[:, c * P:(c + 1) * P],
                                          identity=ident_f[:])
                    return ins

                S.op("pe", trh, reads=["hco", "hco2", "ident_f"], writes=[("ps", bk)])
                for cc in range(4):
                    c = half * 4 + cc
                    S.op("dve", lambda e, c=c, cc=cc, bk=bk: e.tensor_copy(
                        out=uview(t, c, 0, HA),
                        in_=ps[:, bk, cc * P:cc * P + ns * HA].rearrange("p (s j) -> p s j", j=HA)),
                        reads=[("ps", bk)], writes=[("ub", c)])
                    S.op("dve", lambda e, c=c, cc=cc, bk=bk: e.tensor_copy(
                        out=cview(t, c, 0, HB),
                        in_=ps[:, bk, cc * P + P - ns * HB:(cc + 1) * P].rearrange("p (s j) -> p s j", j=HB)),
                        reads=[("ps", bk)], writes=[("cbn", c)])

    sig_slot = {}
    gc_slot = {}

    def stage_win(t, g0, g1):
        NT, nseg, sl = t["NT"], t["nseg"], t["seglen"]
        last = t["last"]
        for g in range(g0, g1):
            sec, h = WIN_GROUPS[g]
            wi, slot, wv = get_w("w_in")
            for i in range(4):
                c = h * 4 + i
                bk = next_bank()

                def mm(e, i=i, bk=bk, wv=wv):
                    ins = None
                    for kc in range(KC):
                        ins = e.matmul(ps[:, bk, 0:NT], lhsT=wv[:, kc, i * P:(i + 1) * P], rhs=hT[:, kc, 0:NT],
                                       start=(kc == 0), stop=(kc == KC - 1))
                    return ins

                S.op("pe", mm, reads=[("ring", slot)] + [("hT", k) for k in range(KC)], writes=[("ps", bk)])
                pin = ps[:, bk, 0:NT]
                if sec == "gate":
                    sl_i = i
                    sig_slot[c] = sl_i
                    S.op("act", lambda e, pin=pin, sl_i=sl_i: e.activation(out=tf[:, sl_i, 0:NT], in_=pin,
                                                                          func=AF.Sigmoid),
                         reads=[("ps", bk)], writes=[("tf", sl_i)])
                elif sec == "val":
                    sl_i = sig_slot[c]
                    S.op("dve", lambda e, pin=pin, sl_i=sl_i, c=c: e.tensor_tensor(
                        out=uview(t, c, HA, sl), in0=pview(t, pin), in1=pview(t, tf[:, sl_i, 0:NT]), op=ALU.mult),
                        reads=[("ps", bk), ("tf", sl_i)], writes=[("ub", c)])
                    if last:
                        S.op("dve", lambda e, pin=pin, sl_i=sl_i, c=c: e.tensor_tensor(
                            out=(uo[:, c, 0:HA] if nseg == 1 else
                                 uo[:, c, :].rearrange("p (s j) -> p s j", j=32)[:, :, 0:HA]),
                            in0=tail(t, pin, HA), in1=tail(t, tf[:, sl_i, 0:NT], HA),
                            op=ALU.mult),
                            reads=[("ps", bk), ("tf", sl_i)], writes=[("tf", 4 + c // 4)])
                elif sec == "gC":
                    sl_i = i
                    gc_slot[c] = sl_i
                    S.op("act", lambda e, pin=pin, sl_i=sl_i: e.activation(out=tf[:, sl_i, 0:NT], in_=pin,
                                                                          func=AF.Copy),
                         reads=[("ps", bk)], writes=[("tf", sl_i)])
                elif sec == "hB":
                    sl_i = gc_slot[c]
                    S.op("dve", lambda e, pin=pin, sl_i=sl_i, c=c: e.tensor_tensor(
                        out=cview(t, c, HB, sl), in0=pview(t, pin), in1=pview(t, tf[:, sl_i, 0:NT]), op=ALU.mult),
                        reads=[("ps", bk), ("tf", sl_i)], writes=[("cbn", c)])
                    if last:
                        S.op("dve", lambda e, pin=pin, sl_i=sl_i, c=c: e.tensor_tensor(
                            out=(uo[:, c, HA:HA + HB] if nseg == 1 else
                                 uo[:, c, :].rearrange("p (s j) -> p s j", j=32)[:, :, HA:HA + HB]),
                            in0=tail(t, pin, HB), in1=tail(t, tf[:, sl_i, 0:NT], HB),
                            op=ALU.mult),
                            reads=[("ps", bk), ("tf", sl_i)], writes=[("tf", 4 + c // 4)])
                elif sec == "gB":
                    S.op("act", lambda e, pin=pin, c=c: e.activation(out=gBm[:, c, 0:NT], in_=pin, func=AF.Copy),
                         reads=[("ps", bk)], writes=[("gBm", c)])
                else:
                    dst = ga if sec == "ga" else gb
                    row = R_GBA if sec == "ga" else R_GBB
                    S.op("act", lambda e, pin=pin, c=c, dst=dst, row=row: e.activation(
                        out=dst[:, c, 0:NT], in_=pin, func=AF.Sigmoid, bias=cvec[:, c, row:row + 1]),
                        reads=[("ps", bk), "cvec"], writes=[(sec, c)])
            release_w(wi)
            if g == 1:
                queue_conv(t, (0, 1))
            if g == 3:
                queue_conv(t, (2, 3))
            if g >= 2:
                if t["idx"] == 0:
                    pump(12 if sec in ("hB", "val") else 16)
                else:
                    pump(7 if sec in ("hB", "val") else 10)

    def stage_cache_out(t):
        nseg = t["nseg"]
        W = 32 * nseg
        for half in range(2):
            bk = next_bank()

            def tro(e, half=half, bk=bk):
                ins = None
                for cc in range(4):
                    c = half * 4 + cc
                    ins = e.transpose(out=ps[0:W, bk, cc * P:(cc + 1) * P], in_=uo[:, c, 0:W], identity=ident_f[:])
                return ins

            S.op("pe", tro, reads=[("tf", 4 + half), "ident_f"], writes=[("ps", bk)])
            S.op("dve", lambda e, half=half, bk=bk: e.tensor_copy(out=hco[0:W, half * 512:(half + 1) * 512],
                                                                 in_=ps[0:W, bk, :]),
                 reads=[("ps", bk)], writes=[("hcoh", half)])
        keys = [("hcoh", 0), ("hcoh", 1)]
        for sg in range(nseg):
            seq = t["seq"] + sg
            S.op("pool", lambda e, sg=sg, seq=seq: e.dma_start(out=oca.ap()[seq * HA:(seq + 1) * HA, :],
                                                               in_=hco[sg * 32:sg * 32 + HA, :]),
                 reads=keys, dma_sem="s_oc")
            S.op("pool", lambda e, sg=sg, seq=seq: e.dma_start(out=ocb.ap()[seq * HB:(seq + 1) * HB, :],
                                                               in_=hco[sg * 32 + HA:sg * 32 + HA + HB, :]),
                 reads=keys, dma_sem="s_oc")
        tok = ("s_oc", S.cnt["s_oc"])
        for k in keys + ["hco", "hco2"]:
            S.readers.setdefault(k, []).append(tok)

    def cab_of(t):
        if t["kpe"] == 0:
            return (lambda c: accb[:, c, :]), (lambda c: ("accb", c))
        return (lambda c: hid[:, c, :]), (lambda c: ("hid", c))

    def convb_block(t):
        NT, sl = t["NT"], t["seglen"]
        for c in range(KC):
            bk2 = next_bank()

            def mmb2(e, c=c, bk2=bk2):
                ins = None
                for k in range(WB):
                    ins = e.matmul(pview(t, ps[:, bk2, 0:NT]), lhsT=dgb[:, c, k, :], rhs=cview(t, c, k, sl),
                                   start=(k == 0), stop=(k == WB - 1))
                return ins

            S.op("pe", mmb2, reads=["dgb", ("cbn", c)], writes=[("ps", bk2)])
            S.op("dve", lambda e, c=c, bk2=bk2: e.tensor_tensor(out=hid[:, 16 + c, 0:NT], in0=ps[:, bk2, 0:NT],
                                                               in1=gBm[:, c, 0:NT], op=ALU.mult),
                 reads=[("ps", bk2), ("gBm", c)], writes=[("hid", 16 + c)])

    def pre_squares(t):
        NT = t["NT"]
        pump(len(conv_q))
        for c in range(6):
            S.op("act", lambda e, c=c: e.activation(out=tb[:, c, 0:NT], in_=accb[:, c, 0:NT], func=AF.Square),
                 reads=[("accb", c)], writes=[("tb", c)])
        t["presq"] = True

    def stage_conv(t):
        NT, nseg, sl = t["NT"], t["nseg"], t["seglen"]
        kpe = t["kpe"]
        cabv, cabk = cab_of(t)
        per_pair = 2 * (WA - kpe)
        n_q0 = len(conv_q)
        bM = next_bank(hold=True)
        bE = next_bank(hold=True)
        pend = None

        def stats(c, sq_slot):
            S.op("pe", lambda e, c=c: e.matmul(ps[:, bM, 0:NT], lhsT=ones_b[:], rhs=cabv(c)[:, 0:NT],
                                               start=(c == 0), stop=(c == KC - 1)),
                 reads=[cabk(c), "ones_b"], writes=[("ps", bM)])
            S.op("pe", lambda e, c=c, sq_slot=sq_slot: e.matmul(ps[:, bE, 0:NT], lhsT=ones_b[:],
                                                               rhs=tb[:, sq_slot, 0:NT],
                                                               start=(c == 0), stop=(c == KC - 1)),
                 reads=[("tb", sq_slot), "ones_b"], writes=[("ps", bE)])

        if kpe == 0 and t.get("presq"):
            pump(len(conv_q))
            for c in range(KC):
                if c >= 6:
                    S.op("act", lambda e, c=c: e.activation(out=tb[:, c % 6, 0:NT], in_=accb[:, c, 0:NT],
                                                           func=AF.Square),
                         reads=[("accb", c)], writes=[("tb", c % 6)])
                stats(c, c % 6)
            t["convb_later"] = True
            return bM, bE

        for c in range(KC):
            need_left = max(0, (KC // 2 - 1 - c // 2) * per_pair)
            if len(conv_q) > need_left:
                pump(len(conv_q) - need_left)
            sq_slot = c % 3
            if kpe == 0:
                S.op("act", lambda e, c=c, sq_slot=sq_slot: e.activation(out=tb[:, sq_slot, 0:NT],
                                                                        in_=accb[:, c, 0:NT], func=AF.Square),
                     reads=[("accb", c)], writes=[("tb", sq_slot)])
            else:
                wi, slot, wv = get_w("dga")
                bk = next_bank()

                def mma(e, c=c, bk=bk, wv=wv):
                    for k in range(kpe):
                        e.matmul(pview(t, ps[:, bk, 0:NT]), lhsT=wv[:, k * P:(k + 1) * P], rhs=uview(t, c, k, sl),
                                 start=(k == 0), stop=False)
                    return e.matmul(ps[:, bk, 0:NT], lhsT=ident_b[:], rhs=accb[:, c, 0:NT], start=False, stop=True)

                S.op("pe", mma, reads=[("ring", slot), ("ub", c), ("accb", c), "ident_b"], writes=[("ps", bk)])
                release_w(wi)
                sq_slot = c % 3
                S.op("act", lambda e, c=c, bk=bk: e.activation(out=hid[:, c, 0:NT], in_=ps[:, bk, 0:NT], func=AF.Identity,
                                                              bias=cvec[:, c, R_DWAB:R_DWAB + 1]),
                     reads=[("ps", bk), "cvec"], writes=[("hid", c)])
                S.op("act", lambda e, c=c, bk=bk, sq_slot=sq_slot: e.activation(
                    out=tb[:, sq_slot, 0:NT], in_=ps[:, bk, 0:NT], func=AF.Square, bias=cvec[:, c, R_DWAB:R_DWAB + 1]),
                    reads=[("ps", bk), "cvec"], writes=[("tb", sq_slot)])
            bk2 = next_bank()

            def mmb(e, c=c, bk2=bk2):
                ins = None
                for k in range(WB):
                    ins = e.matmul(pview(t, ps[:, bk2, 0:NT]), lhsT=dgb[:, c, k, :], rhs=cview(t, c, k, sl),
                                   start=(k == 0), stop=(k == WB - 1))
                return ins

            S.op("pe", mmb, reads=["dgb", ("cbn", c)], writes=[("ps", bk2)])
            S.op("dve", lambda e, c=c, bk2=bk2: e.tensor_tensor(out=hid[:, 16 + c, 0:NT], in0=ps[:, bk2, 0:NT],
                                                               in1=gBm[:, c, 0:NT], op=ALU.mult),
                 reads=[("ps", bk2), ("gBm", c)], writes=[("hid", 16 + c)])
            if pend is not None:
                stats(*pend)
            pend = (c, sq_slot)
        stats(*pend)
        return bM, bE

    def stage_ln(t, bM, bE):
        NT = t["NT"]
        cabv, cabk = cab_of(t)
        S.op("act", lambda e: e.activation(out=dummy[:, 1:2], in_=dummy[:, 0:1], func=AF.Ln), writes=["dummy"])
        S.op("act", lambda e: e.activation(out=tf[:, 0, 0:NT], in_=ps[:, bM, 0:NT], func=AF.Square),
             reads=[("ps", bM)], writes=[("tf", 0)])
        S.op("dve", lambda e: e.tensor_tensor(out=tf[:, 1, 0:NT], in0=ps[:, bE, 0:NT], in1=tf[:, 0, 0:NT],
                                             op=ALU.subtract),
             reads=[("ps", bE), ("tf", 0)], writes=[("tf", 1)])
        S.op("act", lambda e: e.activation(out=tf[:, 0, 0:NT], in_=tf[:, 1, 0:NT], func=AF.Ln, bias=EPS),
             reads=[("tf", 1)], writes=[("tf", 0)])
        S.op("act", lambda e: e.activation(out=tf[:, 1, 0:NT], in_=tf[:, 0, 0:NT], func=AF.Exp, scale=-0.5),
             reads=[("tf", 0)], writes=[("tf", 1)])
        S.op("act", lambda e: e.activation(out=dummy[:, 1:2], in_=dummy[:, 0:1], func=AF.Silu), writes=["dummy"])
        S.op("dve", lambda e: e.scalar_tensor_tensor(out=tf[:, 2, 0:NT], in0=ps[:, bM, 0:NT], scalar=-1.0,
                                                    in1=tf[:, 1, 0:NT], op0=ALU.mult, op1=ALU.mult),
             reads=[("ps", bM), ("tf", 1)], writes=[("tf", 2)])
        unhold(bM)
        unhold(bE)
        if t.get("convb_later"):
            convb_block(t)
        for c in range(KC):
            sl_i = 3 + c % 3
            S.op("dve", lambda e, c=c, sl_i=sl_i: e.tensor_tensor(out=tf[:, sl_i, 0:NT], in0=cabv(c)[:, 0:NT],
                                                                 in1=tf[:, 1, 0:NT], op=ALU.mult),
                 reads=[cabk(c), ("tf", 1)], writes=[("tf", sl_i)])
            S.op("pool" if c % 2 == 0 else "dve", lambda e, sl_i=sl_i: e.tensor_tensor(
                out=tf[:, sl_i, 0:NT], in0=tf[:, sl_i, 0:NT], in1=tf[:, 2, 0:NT], op=ALU.add),
                 reads=[("tf", sl_i), ("tf", 2)], writes=[("tf", sl_i)])
            S.op("act", lambda e, c=c, sl_i=sl_i: e.activation(
                out=hid[:, 8 + c, 0:NT], in_=tf[:, sl_i, 0:NT], func=AF.Silu,
                scale=cvec[:, c, R_LNG:R_LNG + 1], bias=cvec[:, c, R_LNB:R_LNB + 1]),
                reads=[("tf", sl_i), "cvec"], writes=[("hid", 8 + c)])

    def stage_wb_wa(t):
        NT = t["NT"]
        for which in ("wb", "wa"):
            base = 16 if which == "wb" else 8
            for h in range(2):
                wi, slot, wv = get_w(which)
                for i in range(4):
                    c = h * 4 + i
                    bk = next_bank()

                    def mm(e, i=i, bk=bk, wv=wv, base=base):
                        ins = None
                        for kc in range(KC):
                            ins = e.matmul(ps[:, bk, 0:NT], lhsT=wv[:, kc, i * P:(i + 1) * P],
                                           rhs=hid[:, base + kc, 0:NT], start=(kc == 0), stop=(kc == KC - 1))
                        return ins

                    S.op("pe", mm, reads=[("ring", slot)] + [("hid", base + k) for k in range(KC)],
                         writes=[("ps", bk)])
                    if which == "wb":
                        S.op("dve", lambda e, c=c, bk=bk: e.tensor_tensor(out=hid[:, 24 + c, 0:NT],
                                                                         in0=ps[:, bk, 0:NT], in1=gb[:, c, 0:NT],
                                                                         op=ALU.mult),
                             reads=[("ps", bk), ("gb", c)], writes=[("hid", 24 + c)])
                    else:
                        ms = c % 4
                        S.op("dve", lambda e, c=c, bk=bk, ms=ms: e.tensor_tensor(out=tb[:, ms, 0:NT],
                                                                                in0=ps[:, bk, 0:NT],
                                                                                in1=ga[:, c, 0:NT], op=ALU.mult),
                             reads=[("ps", bk), ("ga", c)], writes=[("tb", ms)])
                        S.op("pool", lambda e, c=c, ms=ms: e.tensor_tensor(out=gBm[:, c, 0:NT], in0=tb[:, ms, 0:NT],
                                                                          in1=hid[:, 24 + c, 0:NT], op=ALU.add),
                             reads=[("tb", ms), ("hid", 24 + c)], writes=[("gBm", c)])
                release_w(wi)

    def stage_wo(t):
        buf, ns = t["buf"], t["nsub"]
        for j in range(2):
            wi, slot, wv = get_w("wo")
            for s in range(ns):
                bk = next_bank()

                def mm(e, s=s, bk=bk, wv=wv):
                    ins = None
                    for kc in range(KC):
                        ins = e.matmul(ps[:, bk, :], lhsT=gBm[:, kc, s * P:(s + 1) * P], rhs=wv[:, kc, :],
                                       start=(kc == 0), stop=(kc == KC - 1))
                    return ins

                S.op("pe", mm, reads=[("ring", slot)] + [("gBm", k) for k in range(KC)], writes=[("ps", bk)])
                S.op("dve", lambda e, s=s, j=j, bk=bk: e.tensor_tensor(
                    out=xt[:, buf, s, j * 512:(j + 1) * 512], in0=ps[:, bk, :],
                    in1=xt[:, buf, s, j * 512:(j + 1) * 512], op=ALU.add),
                    reads=[("ps", bk), ("xt", buf, s)], writes=[("xt", buf, s)])
            release_w(wi)

    def stage_wup(t):
        NT = t["NT"]
        rr = 0
        for g in range(8):
            wi, slot, wv = get_w("wup")
            for i in range(4):
                f = g * 4 + i
                bk = next_bank()

                def mm(e, i=i, bk=bk, wv=wv):
                    ins = None
                    for kc in range(KC):
                        ins = e.matmul(ps[:, bk, 0:NT], lhsT=wv[:, kc, i * P:(i + 1) * P], rhs=h2T[:, kc, 0:NT],
                                       start=(kc == 0), stop=(kc == KC - 1))
                    return ins

                S.op("pe", mm, reads=[("ring", slot)] + [("h2T", k) for k in range(KC)], writes=[("ps", bk)])
                S.op("act", lambda e, bk=bk, f=f: e.activation(out=hid[:, f, 0:NT], in_=ps[:, bk, 0:NT],
                                                              func=AF.Relu),
                     reads=[("ps", bk)], writes=[("hid", f)])
                S.op("act", lambda e, f=f: e.activation(out=hid[:, f, 0:NT], in_=hid[:, f, 0:NT], func=AF.Square),
                     reads=[("hid", f)], writes=[("hid", f)])
                pump(2)
            release_w(wi)

    def stage_wdown(t):
        buf, ns = t["buf"], t["nsub"]
        for j in range(2):
            banks = [next_bank(hold=True) for _ in range(ns)]
            for q in range(4):
                wi, slot, wv = get_w("wdn")
                for s in range(ns):
                    bk = banks[s]

                    def mm(e, s=s, bk=bk, wv=wv, q=q):
                        ins = None
                        for fk in range(8):
                            ins = e.matmul(ps[:, bk, :], lhsT=hid[:, q * 8 + fk, s * P:(s + 1) * P], rhs=wv[:, fk, :],
                                           start=(q == 0 and fk == 0), stop=(q == 3 and fk == 7))
                        return ins

                    S.op("pe", mm, reads=[("ring", slot)] + [("hid", q * 8 + fk) for fk in range(8)],
                         writes=[("ps", bk)])
                release_w(wi)
                pump(20 if j == 0 else 0)
            for s in range(ns):
                bk = banks[s]
                S.op("dve", lambda e, s=s, j=j, bk=bk: e.tensor_tensor(
                    out=xt[:, buf, s, j * 512:(j + 1) * 512], in0=ps[:, bk, :],
                    in1=xt[:, buf, s, j * 512:(j + 1) * 512], op=ALU.add),
                    reads=[("ps", bk), ("xt", buf, s)], writes=[("xt", buf, s)])
                unhold(bk)

    def stage_final(t):
        buf, ns, t0, NT = t["buf"], t["nsub"], t["tok0"], t["NT"]
        stage = 2
        for s in range(ns):
            ssap = stat[:, stage, 0, s:s + 1]
            S.op("act", lambda e, s=s, ssap=ssap: e.activation(out=xn[:, s, :], in_=xt[:, buf, s, :], func=AF.Square,
                                                              accum_out=ssap),
                 reads=[("xt", buf, s)], writes=[("st", stage, 0, s), ("xn", s)])
            rsap = rstd_ops(stage, s)
            S.op("dve", lambda e, s=s, rsap=rsap: e.scalar_tensor_tensor(
                out=xt[:, buf, s, :], in0=xt[:, buf, s, :], scalar=rsap, in1=gfin[:], op0=ALU.mult, op1=ALU.mult),
                reads=[("xt", buf, s), ("st", stage, 2, s), "gfin"], writes=[("xt", buf, s)])
        dst = yout.ap()[t0:t0 + NT, :].rearrange("(s p) d -> p s d", p=P)
        S.op("pool", lambda e, dst=dst: e.dma_start(out=dst, in_=xt[:, buf, 0:ns, :]),
             reads=[("xt", buf, s) for s in range(ns)], dma_sem="s_y%d" % buf)

    t0_ = tiles[0]
    rms_part(t0_, 0)
    stage_hist(t0_)
    casts_win()
    casts_attn()
    if KPE0:
        build_diag_a()
    for i in range(min(NSLOT, len(wseq))):
        record_wload(i)
    tr_part(t0_, R_G1, hT, "hT")
    stage_win(t0_, 0, len(WIN_GROUPS))
    GSPLIT = 12
    for t in tiles:
        nxt = tiles[t["idx"] + 1] if t["idx"] + 1 < len(tiles) else None
        if t["last"]:
            stage_cache_out(t)
        bM, bE = stage_conv(t)
        stage_ln(t, bM, bE)
        if t["idx"] == 0:
            casts_up()
        if nxt is not None:
            rms_part(nxt, 0)
        stage_wb_wa(t)
        if nxt is not None:
            tr_part(nxt, R_G1, hT, "hT")
        stage_wo(t)
        if t["idx"] == 0:
            casts_down()
        rms_part(t, 1)
        if nxt is not None:
            stage_hist(nxt)
            stage_win(nxt, 0, GSPLIT)
        tr_part(t, R_G2, h2T, "h2T")
        if nxt is not None:
            stage_win(nxt, GSPLIT, len(WIN_GROUPS))
        stage_wup(t)
        stage_wdown(t)
        if nxt is not None and nxt["kpe"] == 0:
            pre_squares(nxt)
        stage_final(t)
        if t["idx"] + 2 < len(tiles):
            load_x(tiles[t["idx"] + 2])

    S.wait_all("pool", [s for s in sorted(S.sem_names) if s.startswith(("s_y", "s_oc", "s_dbg"))])

    sems = {name: es.enter_context(nc.semaphore(name)) for name in sorted(S.sem_names)}

    def run(stream, e):
        for item in stream:
            if item[0] == "wait":
                e.wait_ge(sems[item[1]], item[2])
            else:
                _, fn, sem, inc = item
                ins = fn(e)
                ins.then_inc(sems[sem], inc)

    with nc.Block() as block:
        @block.tensor
        def _(e):
            run(S.streams["pe"], e)

        @block.scalar
        def _(e):
            run(S.streams["act"], e)

        @block.vector
        def _(e):
            run(S.streams["dve"], e)

        @block.gpsimd
        def _(e):
            run(S.streams["pool"], e)

        @block.sync
        def _(e):
            run(S.streams["sp"], e)
    es.close()
    return nc


def pack_vecs(norm1_g, dw_a, dw_a_b, ln_a_g, ln_a_b, dw_b, gate_bias, norm2_g):
    v = np.zeros((NV, D), np.float32)
    v[R_G1] = norm1_g[0]
    v[R_DWA:R_DWA + WA] = dw_a[0]
    v[R_DWAB] = dw_a_b[0]
    v[R_LNG] = ln_a_g[0]
    v[R_LNB] = ln_a_b[0]
    v[R_DWB:R_DWB + WB] = dw_b[0]
    v[R_GBA] = gate_bias[0][:D]
    v[R_GBB] = gate_bias[0][D:]
    v[R_G2] = norm2_g[0]
    return v


_NC_CACHE = {}


def run_cores(NP, PL, NS, n_cores, x_prompt, x_sample, cache_conv_a, cache_conv_b, norm1_g, w_in, dw_a, dw_a_b,
              ln_a_g, ln_a_b, wa_out, dw_b, wb_out, gate_bias, w_o, norm2_g, w_up, w_down, final_norm_g,
              debug=()):
    f = lambda a: np.ascontiguousarray(np.asarray(a, dtype=np.float32))
    x_prompt, x_sample, cache_conv_a, cache_conv_b = f(x_prompt), f(x_sample), f(cache_conv_a), f(cache_conv_b)
    key = (NP, PL, NS, tuple(debug))
    if key not in _NC_CACHE:
        _NC_CACHE[key] = build(NP, PL, NS, debug)
    nc = _NC_CACHE[key]
    vecs = pack_vecs(f(norm1_g), f(dw_a), f(dw_a_b), f(ln_a_g), f(ln_a_b), f(dw_b), f(gate_bias), f(norm2_g))
    shared = dict(vecs=vecs, gfin=f(final_norm_g).reshape(1, D), ident=np.eye(P, dtype=np.float32),
                  w_in=f(w_in)[0], wa_out=f(wa_out)[0], wb_out=f(wb_out)[0], w_o=f(w_o)[0], w_up=f(w_up)[0],
                  w_down=f(w_down)[0])
    in_maps = []
    for c in range(n_cores):
        xp = x_prompt[c * NP:(c + 1) * NP].reshape(NP * PL, D)
        xs = x_sample[c * NS:(c + 1) * NS].reshape(NS * SL, D)
        m = dict(shared)
        m["xin"] = np.ascontiguousarray(np.concatenate([xp, xs], axis=0))
        m["hca"] = np.ascontiguousarray(cache_conv_a[0, c * NS:(c + 1) * NS].reshape(NS * HA, D))
        m["hcb"] = np.ascontiguousarray(cache_conv_b[0, c * NS:(c + 1) * NS].reshape(NS * HB, D))
        in_maps.append(m)
    res = run_bass_kernel_spmd(nc, in_maps, core_ids=list(range(n_cores)))
    yp, ys, pa, pb, sa, sbb = [], [], [], [], [], []
    for c in range(n_cores):
        r = res.results[c]
        y = np.asarray(r["yout"], dtype=np.float32)
        yp.append(y[:NP * PL].reshape(NP, PL, D))
        ys.append(y[NP * PL:].reshape(NS, SL, D))
        a = np.asarray(r["oca"], dtype=np.float32).reshape(NP + NS, HA, D)
        b = np.asarray(r["ocb"], dtype=np.float32).reshape(NP + NS, HB, D)
        pa.append(a[:NP])
        sa.append(a[NP:])
        pb.append(b[:NP])
        sbb.append(b[NP:])
    out = (np.concatenate(yp, 0), np.concatenate(ys, 0), np.concatenate(pa, 0)[None], np.concatenate(pb, 0)[None],
           np.concatenate(sa, 0)[None], np.concatenate(sbb, 0)[None])
    return out, res


def kernel(**inputs):
    out, _ = run_cores(2, 4096, 4, 8, **inputs)
    return out
```
